# Optimizing a Trainium2 kernel written in Bass

```python
import math
import jax
import jax.numpy as jnp
from jax import lax
import numpy as np

D_MODEL = 1024
BATCH = 8
SEQ = 2048
DEPTH = 2

N_EVEN = (DEPTH + 1) // 2
N_ODD = DEPTH // 2
DEEPNORM_ALPHA = (2 * DEPTH) ** 0.25
DEEPNORM_BETA = (8 * DEPTH) ** -0.25
ROPE_THETA = 10000.0
RMS_EPS = 1e-6
LN_EPS = 1e-5
L2_EPS = 1e-6
POS_OFFSET_MAX = 512

MLA_HEADS = 8
MLA_NOPE = 128
MLA_ROPE = 64
MLA_V = 128
Q_LORA = 768
KV_LORA = 256
Q_BLOCK = 128

DIL_HEADS = 8
DIL_HD = 128
DILATED_GROUPS = ((128, 1), (512, 4), (2048, 16))

GDN_HEADS = 8
GDN_DK = 128
GDN_DV = 128
CONV_K = 5
CHUNK = 64
DT_MIN = 1e-3
DT_MAX = 1e-1

RET_HEADS = 8
RET_DK = 64
RET_DV = 128
RET_DECAY_BASE = 5.0

EVEN_MIX = MLA_HEADS * MLA_V + DIL_HEADS * DIL_HD
EVEN_SPLITS = (Q_LORA, KV_LORA, MLA_ROPE, DIL_HEADS * DIL_HD, DIL_HEADS * DIL_HD, DIL_HEADS * DIL_HD, EVEN_MIX)
EVEN_IN = sum(EVEN_SPLITS)
GDN_CONV_CH = 2 * GDN_HEADS * GDN_DK + GDN_HEADS * GDN_DV
ODD_MIX = GDN_HEADS * GDN_DV + RET_HEADS * RET_DV
ODD_SPLITS = (GDN_CONV_CH, 2 * GDN_HEADS, 2 * GDN_HEADS, RET_HEADS * RET_DK, RET_HEADS * RET_DK, RET_HEADS * RET_DV, ODD_MIX)
ODD_IN = sum(ODD_SPLITS)

kernel_name = 'hybrid_mla_dilated_gdn_retention_encoder'


def split_last(h, sizes):
    offsets = np.cumsum(np.array(sizes))[:-1].tolist()
    return jnp.split(h, offsets, axis=-1)


def rms_norm(x, w):
    xf = x.astype(jnp.float32)
    y = xf * lax.rsqrt(jnp.mean(xf * xf, axis=-1, keepdims=True) + RMS_EPS)
    return (y * w.astype(jnp.float32)).astype(x.dtype)


def layer_norm(x, w, b):
    xf = x.astype(jnp.float32)
    mu = jnp.mean(xf, axis=-1, keepdims=True)
    var = jnp.mean(jnp.square(xf - mu), axis=-1, keepdims=True)
    y = (xf - mu) * lax.rsqrt(var + LN_EPS)
    return (y * w.astype(jnp.float32) + b.astype(jnp.float32)).astype(x.dtype)


def l2_normalize(x):
    xf = x.astype(jnp.float32)
    return xf * lax.rsqrt(jnp.sum(xf * xf, axis=-1, keepdims=True) + L2_EPS)


def apply_rope(x, positions):
    d = x.shape[-1]
    half = d // 2
    inv_freq = ROPE_THETA ** (-jnp.arange(0, d, 2, dtype=jnp.float32) / d)
    ang = positions.astype(jnp.float32)[..., None] * inv_freq
    cos = jnp.cos(ang)[:, :, None, :]
    sin = jnp.sin(ang)[:, :, None, :]
    x1 = x[..., :half].astype(jnp.float32)
    x2 = x[..., half:].astype(jnp.float32)
    return jnp.concatenate([x1 * cos - x2 * sin, x2 * cos + x1 * sin], axis=-1).astype(x.dtype)


def centred_depthwise_conv(x, w):
    k, c = w.shape
    return lax.conv_general_dilated(
        x, w[:, None, :].astype(x.dtype), window_strides=(1,),
        padding=[((k - 1) // 2, k // 2)],
        dimension_numbers=('NWC', 'WIO', 'NWC'), feature_group_count=c)


def mla_attention(q_nope, q_rope, k_nope, k_rope, v):
    bsz, seq, heads, _ = q_nope.shape
    nblk = seq // Q_BLOCK
    scale = (MLA_NOPE + MLA_ROPE) ** -0.5

    def to_blocks(t):
        return jnp.moveaxis(t.reshape((bsz, nblk, Q_BLOCK) + t.shape[2:]), 1, 0)

    def attend(blk):
        qn, qr = blk
        s = (jnp.einsum('bqhd,bkhd->bhqk', qn, k_nope, preferred_element_type=jnp.float32)
             + jnp.einsum('bqhr,bkr->bhqk', qr, k_rope, preferred_element_type=jnp.float32))
        p = jax.nn.softmax(s * scale, axis=-1)
        return jnp.einsum('bhqk,bkhd->bqhd', p.astype(v.dtype), v)

    o = lax.map(attend, (to_blocks(q_nope), to_blocks(q_rope)))
    return jnp.moveaxis(o, 0, 1).reshape(bsz, seq, heads * v.shape[-1])


def banded_window_attention(q, k, v, radius):
    n, length, d = q.shape
    blk = radius
    nb = -(-length // blk)
    pad = nb * blk - length
    qb = jnp.pad(q, ((0, 0), (0, pad), (0, 0))).reshape(n, nb, blk, d)

    def band(t):
        tp = jnp.pad(t, ((0, 0), (blk, pad + blk), (0, 0))).reshape(n, nb + 2, blk, d)
        return jnp.concatenate([tp[:, :-2], tp[:, 1:-1], tp[:, 2:]], axis=2)

    kb, vb = band(k), band(v)
    qpos = jnp.arange(nb)[:, None] * blk + jnp.arange(blk)[None, :]
    kpos = (jnp.arange(nb)[:, None] - 1) * blk + jnp.arange(3 * blk)[None, :]
    rel = kpos[:, None, :] - qpos[:, :, None]
    valid = (jnp.abs(rel) <= radius) & (kpos[:, None, :] >= 0) & (kpos[:, None, :] < length)
    s = jnp.einsum('nbqd,nbkd->nbqk', qb, kb, preferred_element_type=jnp.float32) * (d ** -0.5)
    s = jnp.where(valid, s, -jnp.inf)
    lse = jax.nn.logsumexp(s, axis=-1)
    p = jnp.exp(s - lse[..., None])
    o = jnp.einsum('nbqk,nbkd->nbqd', p, vb.astype(jnp.float32))
    return o.reshape(n, nb * blk, d)[:, :length], lse.reshape(n, nb * blk)[:, :length]


def dilated_window_attention(q, k, v):
    bsz, seq, heads, dh = q.shape
    outs, lses = [], []
    for window, dil in DILATED_GROUPS:
        radius = window // (2 * dil)
        length = seq // dil

        def to_sub(t):
            return t.reshape(bsz, length, dil, heads, dh).transpose(0, 2, 3, 1, 4).reshape(bsz * dil * heads, length, dh)

        o, lse = banded_window_attention(to_sub(q), to_sub(k), to_sub(v), radius)
        outs.append(o.reshape(bsz, dil, heads, length, dh).transpose(0, 3, 1, 2, 4).reshape(bsz, seq, heads, dh))
        lses.append(lse.reshape(bsz, dil, heads, length).transpose(0, 3, 1, 2).reshape(bsz, seq, heads))
    weights = jax.nn.softmax(jnp.stack(lses), axis=0)
    o = jnp.sum(weights[..., None] * jnp.stack(outs), axis=0)
    return o.astype(q.dtype).reshape(bsz, seq, heads * dh)


def gated_delta_rule_chunked(q, k, v, g, beta):
    bsz, heads, seq, dk = q.shape
    dv = v.shape[-1]
    n = seq // CHUNK
    q = q.astype(jnp.float32).reshape(bsz, heads, n, CHUNK, dk)
    k = k.astype(jnp.float32).reshape(bsz, heads, n, CHUNK, dk)
    v = v.astype(jnp.float32).reshape(bsz, heads, n, CHUNK, dv)
    g = g.astype(jnp.float32).reshape(bsz, heads, n, CHUNK)
    beta = beta.astype(jnp.float32).reshape(bsz, heads, n, CHUNK)
    gc = jnp.cumsum(g, axis=-1)
    causal = jnp.tril(jnp.ones((CHUNK, CHUNK), dtype=bool))
    eye = jnp.eye(CHUNK, dtype=jnp.float32)
    decay = jnp.exp(jnp.where(causal, gc[..., :, None] - gc[..., None, :], -jnp.inf))
    kb = k * beta[..., None]
    lower = jnp.tril(jnp.einsum('bhncd,bhnsd->bhncs', kb, k) * decay, -1)
    t_mat = lax.linalg.triangular_solve(lower + eye, jnp.broadcast_to(eye, lower.shape),
                                        left_side=True, lower=True, unit_diagonal=True)
    w = jnp.einsum('bhncs,bhnsd->bhncd', t_mat, kb * jnp.exp(gc)[..., None])
    u = jnp.einsum('bhncs,bhnse->bhnce', t_mat, v * beta[..., None])
    attn = jnp.einsum('bhncd,bhnsd->bhncs', q, k) * decay
    qg = q * jnp.exp(gc)[..., None]
    kd = k * jnp.exp(gc[..., -1:] - gc)[..., None]
    g_last = jnp.exp(gc[..., -1])

    def step(state, xs):
        w_i, u_i, attn_i, qg_i, kd_i, gl_i = xs
        v_new = u_i - jnp.einsum('bhcd,bhde->bhce', w_i, state)
        o_i = jnp.einsum('bhcd,bhde->bhce', qg_i, state) + jnp.einsum('bhcs,bhse->bhce', attn_i, v_new)
        state = state * gl_i[..., None, None] + jnp.einsum('bhcd,bhce->bhde', kd_i, v_new)
        return state, o_i

    xs = tuple(jnp.moveaxis(t, 2, 0) for t in (w, u, attn, qg, kd, g_last))
    _, o = lax.scan(step, jnp.zeros((bsz, heads, dk, dv), jnp.float32), xs)
    return jnp.moveaxis(o, 0, 2).reshape(bsz, heads, seq, dv)


def retention_chunked(q, k, v, inclusive):
    bsz, heads, seq, dk = q.shape
    dv = v.shape[-1]
    n = seq // CHUNK
    log_gamma = jnp.log1p(-(2.0 ** (-RET_DECAY_BASE - jnp.arange(heads, dtype=jnp.float32))))
    q = q.astype(jnp.float32).reshape(bsz, heads, n, CHUNK, dk)
    k = k.astype(jnp.float32).reshape(bsz, heads, n, CHUNK, dk)
    v = v.astype(jnp.float32).reshape(bsz, heads, n, CHUNK, dv)
    idx = jnp.arange(CHUNK, dtype=jnp.float32)
    rel = idx[:, None] - idx[None, :]
    mask = rel >= 0 if inclusive else rel > 0
    dmat = jnp.where(mask, jnp.exp(log_gamma[:, None, None] * jnp.maximum(rel, 0.0)), 0.0)
    inner = jnp.einsum('bhncs,bhnse->bhnce', jnp.einsum('bhncd,bhnsd->bhncs', q, k) * dmat[:, None], v)
    xi = jnp.exp(log_gamma[:, None] * (idx + 1.0))
    zeta = jnp.exp(log_gamma[:, None] * (CHUNK - 1.0 - idx))
    kv_chunk = jnp.einsum('bhncd,bhnce->bhnde', k * zeta[:, None, :, None], v)
    qx = q * xi[:, None, :, None]
    g_chunk = jnp.exp(log_gamma * CHUNK)

    def step(state, xs):
        qx_i, kv_i = xs
        o_i = jnp.einsum('bhcd,bhde->bhce', qx_i, state)
        state = state * g_chunk[:, None, None] + kv_i
        return state, o_i

    _, cross = lax.scan(step, jnp.zeros((bsz, heads, dk, dv), jnp.float32),
                        (jnp.moveaxis(qx, 2, 0), jnp.moveaxis(kv_chunk, 2, 0)))
    return (inner + jnp.moveaxis(cross, 0, 2)).reshape(bsz, heads, seq, dv)


def head_group_norm(o, w, b):
    mu = jnp.mean(o, axis=-1, keepdims=True)
    var = jnp.mean(jnp.square(o - mu), axis=-1, keepdims=True)
    o = (o - mu) * lax.rsqrt(var + LN_EPS)
    bsz, heads, seq, dv = o.shape
    return o.transpose(0, 2, 1, 3).reshape(bsz, seq, heads * dv) * w.astype(jnp.float32) + b.astype(jnp.float32)


def mla_dilated_layer(x, positions, w_in, q_norm, w_uq, kv_norm, w_ukv, w_out, ln_w, ln_b):
    bsz, seq, _ = x.shape
    c_q, c_kv, k_rope, q_b, k_b, v_b, z = split_last(x @ w_in, EVEN_SPLITS)
    q_a = (rms_norm(c_q, q_norm) @ w_uq).reshape(bsz, seq, MLA_HEADS, MLA_NOPE + MLA_ROPE)
    q_nope = q_a[..., :MLA_NOPE]
    q_rope = apply_rope(q_a[..., MLA_NOPE:], positions)
    k_rope = apply_rope(k_rope[:, :, None, :], positions)[:, :, 0, :]
    kv = (rms_norm(c_kv, kv_norm) @ w_ukv).reshape(bsz, seq, MLA_HEADS, MLA_NOPE + MLA_V)
    o_a = mla_attention(q_nope, q_rope, kv[..., :MLA_NOPE], k_rope, kv[..., MLA_NOPE:])
    def heads(t):
        return t.reshape(bsz, seq, DIL_HEADS, DIL_HD)
    o_b = dilated_window_attention(apply_rope(heads(q_b), positions), apply_rope(heads(k_b), positions), heads(v_b))
    y = jnp.concatenate([o_a, o_b], axis=-1) * jax.nn.silu(z)
    return layer_norm(DEEPNORM_ALPHA * x + y @ w_out, ln_w, ln_b)


def delta_retention_layer(x, positions, w_in, conv_w, a_log, dt_bias, gdn_norm, ret_norm_w, ret_norm_b, w_out, ln_w, ln_b):
    bsz, seq, _ = x.shape
    qkv_c, beta_raw, a_raw, q_d, k_d, v_d, z = split_last(x @ w_in, ODD_SPLITS)

    def to_bhsd(t, d):
        return t.reshape(bsz, seq, -1, d).transpose(0, 2, 1, 3)

    def flip(t):
        return jnp.flip(t, axis=2)

    qkv_c = jax.nn.silu(centred_depthwise_conv(qkv_c, conv_w))
    q_c, k_c, v_c = split_last(qkv_c, (GDN_HEADS * GDN_DK, GDN_HEADS * GDN_DK, GDN_HEADS * GDN_DV))
    q_c = l2_normalize(to_bhsd(q_c, GDN_DK)) * (GDN_DK ** -0.5)
    k_c = l2_normalize(to_bhsd(k_c, GDN_DK))
    v_c = to_bhsd(v_c, GDN_DV)
    beta = jax.nn.sigmoid(beta_raw.astype(jnp.float32)).reshape(bsz, seq, 2, GDN_HEADS).transpose(2, 0, 3, 1)
    a_in = a_raw.astype(jnp.float32).reshape(bsz, seq, 2, GDN_HEADS).transpose(2, 0, 3, 1)
    g = -jnp.exp(a_log.astype(jnp.float32))[:, None, :, None] * jax.nn.softplus(a_in + dt_bias.astype(jnp.float32)[:, None, :, None])
    o_fwd = gated_delta_rule_chunked(q_c, k_c, v_c, g[0], beta[0])
    o_bwd = flip(gated_delta_rule_chunked(flip(q_c), flip(k_c), flip(v_c), flip(g[1]), flip(beta[1])))
    o_c = rms_norm(o_fwd + o_bwd, gdn_norm).transpose(0, 2, 1, 3).reshape(bsz, seq, GDN_HEADS * GDN_DV)
    q_r = apply_rope(q_d.reshape(bsz, seq, RET_HEADS, RET_DK), positions).transpose(0, 2, 1, 3)
    k_r = (apply_rope(k_d.reshape(bsz, seq, RET_HEADS, RET_DK), positions) * (RET_DK ** -0.5)).transpose(0, 2, 1, 3)
    v_r = to_bhsd(v_d, RET_DV)
    o_d = retention_chunked(q_r, k_r, v_r, True) + flip(retention_chunked(flip(q_r), flip(k_r), flip(v_r), False))
    o_d = head_group_norm(o_d, ret_norm_w, ret_norm_b)
    y = jnp.concatenate([o_c.astype(x.dtype), o_d.astype(x.dtype)], axis=-1) * jax.nn.silu(z)
    return layer_norm(DEEPNORM_ALPHA * x + y @ w_out, ln_w, ln_b)


def setup_inputs(seed: int = 0) -> dict:
    key = jax.random.key(seed)
    ks = jax.random.split(key, 24)
    f32 = jnp.float32

    def dense(k, shape, fan_in, gain=1.0):
        return jax.random.normal(k, shape, f32) * (gain * fan_in ** -0.5)

    def norm_gain(k, shape):
        return 1.0 + 0.02 * jax.random.normal(k, shape, f32)

    def small_bias(k, shape):
        return 0.02 * jax.random.normal(k, shape, f32)

    x = jax.random.normal(ks[0], (BATCH, SEQ, D_MODEL), f32)
    positions = (jnp.arange(SEQ, dtype=jnp.int32)[None, :]
                 + jax.random.randint(ks[1], (BATCH, 1), 0, POS_OFFSET_MAX, dtype=jnp.int32))
    dt = jnp.exp(jax.random.uniform(ks[12], (N_ODD, 2, GDN_HEADS), f32, math.log(DT_MIN), math.log(DT_MAX)))
    return {
        'x': x,
        'positions': positions,
        'even_w_in': dense(ks[2], (N_EVEN, D_MODEL, EVEN_IN), D_MODEL),
        'even_q_norm': norm_gain(ks[3], (N_EVEN, Q_LORA)),
        'even_w_uq': dense(ks[4], (N_EVEN, Q_LORA, MLA_HEADS * (MLA_NOPE + MLA_ROPE)), Q_LORA),
        'even_kv_norm': norm_gain(ks[5], (N_EVEN, KV_LORA)),
        'even_w_ukv': dense(ks[6], (N_EVEN, KV_LORA, MLA_HEADS * (MLA_NOPE + MLA_V)), KV_LORA),
        'even_w_out': dense(ks[7], (N_EVEN, EVEN_MIX, D_MODEL), EVEN_MIX, DEEPNORM_BETA),
        'even_ln_w': norm_gain(ks[8], (N_EVEN, D_MODEL)),
        'even_ln_b': small_bias(ks[9], (N_EVEN, D_MODEL)),
        'odd_w_in': dense(ks[10], (N_ODD, D_MODEL, ODD_IN), D_MODEL),
        'odd_conv_w': dense(ks[11], (N_ODD, CONV_K, GDN_CONV_CH), CONV_K),
        'odd_a_log': jnp.log(jax.random.uniform(ks[13], (N_ODD, 2, GDN_HEADS), f32, 1.0, 16.0)),
        'odd_dt_bias': dt + jnp.log(-jnp.expm1(-dt)),
        'odd_gdn_norm': norm_gain(ks[14], (N_ODD, GDN_DV)),
        'odd_ret_norm_w': norm_gain(ks[15], (N_ODD, RET_HEADS * RET_DV)),
        'odd_ret_norm_b': small_bias(ks[16], (N_ODD, RET_HEADS * RET_DV)),
        'odd_w_out': dense(ks[17], (N_ODD, ODD_MIX, D_MODEL), ODD_MIX, DEEPNORM_BETA),
        'odd_ln_w': norm_gain(ks[18], (N_ODD, D_MODEL)),
        'odd_ln_b': small_bias(ks[19], (N_ODD, D_MODEL)),
    }


def reference(x, positions, even_w_in, even_q_norm, even_w_uq, even_kv_norm, even_w_ukv, even_w_out,
              even_ln_w, even_ln_b, odd_w_in, odd_conv_w, odd_a_log, odd_dt_bias, odd_gdn_norm,
              odd_ret_norm_w, odd_ret_norm_b, odd_w_out, odd_ln_w, odd_ln_b):
    for layer in range(DEPTH):
        i = layer // 2
        if layer % 2 == 0:
            x = mla_dilated_layer(x, positions, even_w_in[i], even_q_norm[i], even_w_uq[i], even_kv_norm[i],
                                  even_w_ukv[i], even_w_out[i], even_ln_w[i], even_ln_b[i])
        else:
            x = delta_retention_layer(x, positions, odd_w_in[i], odd_conv_w[i], odd_a_log[i], odd_dt_bias[i],
                                      odd_gdn_norm[i], odd_ret_norm_w[i], odd_ret_norm_b[i], odd_w_out[i],
                                      odd_ln_w[i], odd_ln_b[i])
    return x
```

```python
import math
from contextlib import ExitStack
import numpy as np
import concourse.bass as bass
import concourse.mybir as mybir
from concourse.bass_utils import run_bass_kernel_spmd

F32 = mybir.dt.float32
BF16 = mybir.dt.bfloat16
I32 = mybir.dt.int32
ALU = mybir.AluOpType
AF = mybir.ActivationFunctionType

S = 2048
D = 1024
NT = 16
ALPHA = 4.0 ** 0.25
THETA = 10000.0
PI = math.pi


class V:
    def __init__(self, t, ap):
        self.t = t
        self.ap = ap

    def __getitem__(self, k):
        return V(self.t, self.ap[k])


class T(V):
    def __init__(self, ap, name=""):
        self.t = self
        self.ap = ap
        self.name = name
        self.w = None
        self.rd = {}
        self.rd_dma = []
        self.dma_ops = []
        self.sem = None


class Op:
    __slots__ = ("eng", "fn", "deps", "signal", "cnt", "is_dma", "dbuf", "dval", "group")

    def __init__(self, eng, fn):
        self.eng = eng
        self.fn = fn
        self.deps = set()
        self.signal = False
        self.cnt = None
        self.is_dma = False
        self.dbuf = None
        self.dval = 0
        self.group = None


SEG = 8000


class Prog:
    ENGS = ("pe", "act", "dve", "pool", "sp")

    def __init__(self):
        self.ops = []
        self.last = {e: None for e in self.ENGS}
        self.groups = {}

    def add(self, eng, fn, r=(), w=(), dbuf=None, group=None):
        i = len(self.ops)
        op = Op(eng, fn)
        is_dma = dbuf is not None or group is not None
        op.is_dma = is_dma
        deps = op.deps
        for v in r:
            t = v.t
            if t.w is not None:
                deps.add(t.w)
        for v in w:
            t = v.t
            if t.w is not None:
                deps.add(t.w)
            deps.update(t.rd.values())
            deps.update(t.rd_dma)
        if dbuf is not None:
            if dbuf.dma_ops:
                deps.add(dbuf.dma_ops[-1])
            dbuf.dma_ops.append(i)
            op.dbuf = dbuf
            op.dval = 16 * len(dbuf.dma_ops)
        if group is not None:
            g = self.groups.setdefault(group, [])
            g.append(i)
            op.group = group
        for v in r:
            t = v.t
            if is_dma:
                t.rd_dma.append(i)
            else:
                t.rd[eng] = i
        for v in w:
            t = v.t
            t.w = i
            t.rd = {}
            t.rd_dma = []
        self.ops.append(op)
        if fn is not None:
            self.last[eng] = i
        return i

    def barrier(self):
        lasts = [v for v in self.last.values() if v is not None]
        for e in ("pe", "act", "dve", "pool", "sp"):
            i = self.add(e, None)
            self.ops[i].deps.update(x for x in lasts)

    def matmul(self, out, lhsT, rhs, start=True, stop=True, **kw):
        return self.add("pe", lambda e: e.matmul(out.ap, lhsT.ap, rhs.ap, start=start, stop=stop, **kw),
                        r=[lhsT, rhs], w=[out])

    def transpose(self, out, in_, ident):
        return self.add("pe", lambda e: e.transpose(out.ap, in_.ap, ident.ap), r=[in_, ident], w=[out])

    def act(self, out, in_, func, bias=0.0, scale=1.0, extra_r=()):
        b = bias.ap if isinstance(bias, V) else bias
        sc = scale.ap if isinstance(scale, V) else scale
        r = [in_] + [x for x in (bias, scale) if isinstance(x, V)] + list(extra_r)
        return self.add("act", lambda e: e.activation(out.ap, in_.ap, func, bias=b, scale=sc), r=r, w=[out])

    def ts(self, eng, out, in0, s1, s2, op0, op1=None):
        a1 = s1.ap if isinstance(s1, V) else s1
        a2 = s2.ap if isinstance(s2, V) else s2
        r = [in0] + [x for x in (s1, s2) if isinstance(x, V)]
        if op1 is None and eng == "pool" and op0 == ALU.mult:
            return self.add(eng, lambda e: e.tensor_scalar(out.ap, in0.ap, a1, 0.0, ALU.mult, ALU.add), r=r, w=[out])
        if op1 is None:
            return self.add(eng, lambda e: e.tensor_scalar(out.ap, in0.ap, a1, None, op0), r=r, w=[out])
        return self.add(eng, lambda e: e.tensor_scalar(out.ap, in0.ap, a1, a2, op0, op1), r=r, w=[out])

    def tt(self, eng, out, in0, in1, op):
        return self.add(eng, lambda e: e.tensor_tensor(out.ap, in0.ap, in1.ap, op), r=[in0, in1], w=[out])

    def stt(self, out, in0, sc, in1, op0, op1):
        a = sc.ap if isinstance(sc, V) else sc
        r = [in0, in1] + ([sc] if isinstance(sc, V) else [])
        return self.add("dve", lambda e: e.scalar_tensor_tensor(out.ap, in0.ap, a, in1.ap, op0, op1), r=r, w=[out])

    def copy(self, eng, out, in_):
        if eng == "act":
            return self.add("act", lambda e: e.copy(out.ap, in_.ap), r=[in_], w=[out])
        return self.add(eng, lambda e: e.tensor_copy(out.ap, in_.ap), r=[in_], w=[out])

    def recip(self, out, in_):
        return self.add("dve", lambda e: e.reciprocal(out.ap, in_.ap), r=[in_], w=[out])

    def memset(self, eng, out, val):
        return self.add(eng, lambda e: e.memset(out.ap, val), w=[out])

    def dma(self, out_ap, in_ap, r=(), w=(), dbuf=None, group=None, eng="sp", **kw):
        return self.add(eng, lambda e: e.dma_start(out=out_ap, in_=in_ap, **kw), r=r, w=w, dbuf=dbuf, group=group)

    def emit(self, nc):
        ops = self.ops
        for i, op in enumerate(ops):
            for d in op.deps:
                dop = ops[d]
                if dop.is_dma:
                    continue
                if dop.eng == "pe" and op.eng == "pe":
                    continue
                dop.signal = True
        cnt = {e: 0 for e in self.ENGS}
        for op in ops:
            if op.signal and not op.is_dma:
                if op.fn is None:
                    raise RuntimeError("barrier op cannot signal")
                cnt[op.eng] += 1
                op.cnt = cnt[op.eng]
        with ExitStack() as es:
            sems = {}
            for e in self.ENGS:
                n = (cnt[e] + SEG - 1) // SEG
                sems[e] = [es.enter_context(nc.semaphore(f"s_{e}{k}")) for k in range(n)]
            dsem = {}
            gsem = {}
            for op in ops:
                if op.dbuf is not None and id(op.dbuf) not in dsem:
                    dsem[id(op.dbuf)] = es.enter_context(nc.semaphore(f"d{len(dsem)}"))
                if op.group is not None and op.group not in gsem:
                    gsem[op.group] = es.enter_context(nc.semaphore(f"g{len(gsem)}"))
            self.n_sems = sum(len(v) for v in sems.values()) + len(dsem) + len(gsem)
            block = es.enter_context(nc.Block())

            def run(ename, eng):
                waited = {}
                for i, op in enumerate(ops):
                    if op.eng != ename:
                        continue
                    for d in sorted(op.deps):
                        dop = ops[d]
                        if dop.is_dma:
                            if dop.dbuf is not None:
                                key = ("d", id(dop.dbuf))
                                sem = dsem[id(dop.dbuf)]
                                val = dop.dval
                            else:
                                key = ("g", dop.group)
                                sem = gsem[dop.group]
                                val = 16 * len(self.groups[dop.group])
                        else:
                            if dop.eng == "pe" and ename == "pe":
                                continue
                            c = dop.cnt - 1
                            key = (dop.eng, c // SEG)
                            sem = sems[dop.eng][c // SEG]
                            val = (c % SEG) + 1
                        if waited.get(key, 0) >= val:
                            continue
                        waited[key] = val
                        eng.wait_ge(sem, val)
                    if op.fn is None:
                        continue
                    ins = op.fn(eng)
                    if op.is_dma:
                        if op.dbuf is not None:
                            ins.then_inc(dsem[id(op.dbuf)], 16)
                        else:
                            ins.then_inc(gsem[op.group], 16)
                    elif op.signal:
                        c = op.cnt - 1
                        ins.then_inc(sems[ename][c // SEG], 1)

            @block.tensor
            def _(eng):
                run("pe", eng)

            @block.scalar
            def _(eng):
                run("act", eng)

            @block.vector
            def _(eng):
                run("dve", eng)

            @block.gpsimd
            def _(eng):
                run("pool", eng)

            @block.sync
            def _(eng):
                run("sp", eng)


class Builder:
    def __init__(self, nc, debug_layers=2):
        self.nc = nc
        self.P = Prog()
        self.es = ExitStack()
        self.n = 0
        self.debug_layers = debug_layers

    def sb(self, shape, dt, name=None, stack=None):
        self.n += 1
        nm = f"{name or 't'}_{self.n}"
        h = (stack or self.es).enter_context(self.nc.sbuf_tensor(nm, list(shape), dt))
        return T(h[:] if len(shape) == 2 else h[tuple(slice(None) for _ in shape)], nm)

    def ps(self, shape, dt, name=None):
        self.n += 1
        nm = f"{name or 'p'}_{self.n}"
        h = self.es.enter_context(self.nc.psum_tensor(nm, list(shape), dt))
        return T(h[tuple(slice(None) for _ in shape)], nm)

    def dram_in(self, name, shape, dt):
        return self.nc.dram_tensor(name, list(shape), dt, kind="ExternalInput").ap()

    def setup_common(self):
        nc, P = self.nc, self.P
        self.x_d = self.dram_in("x", [S, D], F32)
        self.pos_d = self.dram_in("pos", [1, S], I32)
        self.out_d = nc.dram_tensor("out", [S, D], F32, kind="ExternalOutput").ap()
        self.c_ident = self.dram_in("c_ident", [128, 128], F32)
        self.c_invf = self.dram_in("c_invf", [128, 2], F32)
        self.c_sgn = self.dram_in("c_sgn", [128, 2], F32)
        self.c_perm = self.dram_in("c_perm", [128, 256], F32)

        self.xT = self.sb([128, 8, S], BF16, "xT")
        self.xT_tb = [T(self.xT.ap[:, :, tb * 512:(tb + 1) * 512], f"xT{tb}") for tb in range(4)]
        self.acc = self.sb([128, NT, D], F32, "acc")
        self.acc_t = [T(self.acc.ap[:, i, :], f"acc{i}") for i in range(NT)]
        self.ident_f = self.sb([128, 128], F32, "identf")
        self.ident_b = self.sb([128, 128], BF16, "identb")
        self.ones_b = self.sb([128, 128], BF16, "ones")
        self.invf = self.sb([128, 2], F32, "invf")
        self.sgn = self.sb([128, 2], F32, "sgn")
        self.perm_f = self.sb([128, 256], F32, "permf")
        self.perm_b = self.sb([128, 256], BF16, "permb")
        self.wst = [self.sb([128, 1152], F32, f"wst{i}") for i in range(2)]
        self.wbf = [self.sb([128, 1152], BF16, f"wbf{i}") for i in range(2)]
        self.wst_i = 0
        self.wbf_i = 0
        self.ztmp = [self.sb([128, 512], F32, f"ztmp{i}") for i in range(2)]
        self.pb = [self.ps([128, 512], F32, f"pb{i}") for i in range(8)]

        P.dma(self.ident_f.ap, self.c_ident, w=[self.ident_f], group="const")
        P.dma(self.invf.ap, self.c_invf, w=[self.invf], group="const")
        P.dma(self.sgn.ap, self.c_sgn, w=[self.sgn], group="const")
        P.dma(self.perm_f.ap, self.c_perm, w=[self.perm_f], group="const")
        P.copy("dve", self.ident_b, self.ident_f)
        P.copy("dve", self.perm_b, self.perm_f)
        P.memset("dve", self.ones_b, 1.0)

    def rope_tables(self, col, st):
        P = self.P
        self.cos_t = self.sb([128, S], F32, "cos", st)
        self.sin_t = self.sb([128, S], F32, "sin", st)
        st2 = ExitStack()
        t = self.sb([128, 512], F32, "rt_t", st2)
        nf = self.sb([128, 512], F32, "rt_f", st2)
        pi_ = self.sb([128, 512], I32, "rt_pi", st2)
        pf_ = self.sb([128, 512], F32, "rt_pf", st2)
        for tb in range(4):
            sl = slice(tb * 512, (tb + 1) * 512)
            P.dma(pi_.ap, self.pos_d.partition_broadcast(128)[:, 0, sl], w=[pi_], dbuf=pi_)
            P.copy("dve", pf_, pi_)
            for which, off in ((self.sin_t, 0.0), (self.cos_t, 0.25)):
                P.ts("dve", t, pf_, self.invf[:, col:col + 1], 1.0 / (2 * PI), ALU.mult, ALU.mult)
                if off:
                    P.ts("dve", t, t, off, None, ALU.add)
                P.ts("dve", nf, t, 8388608.0, None, ALU.add)
                P.ts("dve", nf, nf, -8388608.0, None, ALU.add)
                P.tt("dve", t, t, nf, ALU.subtract)
                k = 1.0 - 1e-6
                P.act(which[:, sl], t, AF.Sin, scale=2 * PI * k)
        P.ts("dve", self.sin_t, self.sin_t, self.sgn[:, col:col + 1], None, ALU.mult)
        P.barrier()
        st2.close()

    def load_w(self, src_ap, shape, eng_cast="pool"):
        P = self.P
        n = int(np.prod(shape[1:]))
        assert n <= 1152, shape
        st = self.wst[self.wst_i % 2]
        self.wst_i += 1
        wb = self.wbf[self.wbf_i % 2]
        self.wbf_i += 1
        if len(shape) == 3:
            sv = st.ap[:, 0:n].rearrange("p (a b) -> p a b", a=shape[1])
            wv = wb.ap[:, 0:n].rearrange("p (a b) -> p a b", a=shape[1])
        else:
            sv = st.ap[:, 0:n]
            wv = wb.ap[:, 0:n]
        if shape[0] < 128:
            sv = sv[0:shape[0]]
            wv = wv[0:shape[0]]
        P.dma(sv, src_ap, w=[st], dbuf=st)
        P.add(eng_cast, lambda e: e.tensor_copy(wv, sv), r=[st], w=[wb])
        return V(wb, wv)

    def load_xT(self, first):
        P = self.P
        st = ExitStack()
        xin = [self.sb([128, D], F32, f"xin{i}", st) for i in range(2)]
        xbf = [self.sb([128, D], BF16, f"xbf{i}", st) for i in range(2)]
        tp = [self.pb[6], self.pb[7]]
        for i in range(NT):
            if first:
                src = xin[i % 2]
                P.dma(src.ap, self.x_d[i * 128:(i + 1) * 128, :], w=[src], dbuf=src)
                P.ts("pool", self.acc_t[i], src, ALPHA, None, ALU.mult)
            else:
                src = self.acc_t[i]
            xb = xbf[i % 2]
            P.copy("act", xb, src)
            for half in range(2):
                pbk = tp[half]
                pv = V(pbk, pbk.ap.bitcast(BF16))
                for c in range(4):
                    cc = half * 4 + c
                    P.transpose(pv[:, c * 128:(c + 1) * 128], xb[:, cc * 128:(cc + 1) * 128], self.ident_b)
                dst = self.xT_tb[i // 4][:, half * 4:(half + 1) * 4, (i % 4) * 128:(i % 4 + 1) * 128]
                srcv = V(pbk, pbk.ap.bitcast(BF16)[:, 0:512].rearrange("p (c t) -> p c t", c=4))
                P.copy("dve", dst, srcv)
            if not first:
                P.ts("pool", self.acc_t[i], self.acc_t[i], ALPHA, None, ALU.mult)
        P.barrier()
        st.close()

    def proj_fm(self, ps_out, w, ncols_sl, tb, nk=8, src=None):
        P = self.P
        for k in range(nk):
            rhs = (self.xT_tb[tb][:, k, :] if src is None else src[:, k, tb * 512:(tb + 1) * 512])
            P.matmul(ps_out, w[:, k, ncols_sl], rhs, start=(k == 0), stop=(k == nk - 1))

    def rope_apply(self, dst, ps_x, tb, d, st_tmp, perm=None):
        P = self.P
        xb, ps_r, t1, t2 = st_tmp
        if perm is None:
            perm = self.perm_b[:, 0:128] if d == 128 else self.perm_b[0:64, 128:192]
        tsl = slice(tb * 512, (tb + 1) * 512)
        P.copy("dve", xb[0:d, :], ps_x)
        P.matmul(ps_r[0:d, :], perm, xb[0:d, :])
        P.tt("dve", t1[0:d, :], ps_x, self.cos_t[0:d, tsl], ALU.mult)
        P.tt("dve", t2[0:d, :], ps_r[0:d, :], self.sin_t[0:d, tsl], ALU.mult)
        P.tt("pool", dst, t1[0:d, :], t2[0:d, :], ALU.add)

    def rope_proj4(self, dst, w, wsl, nk, src, d, perm=None):
        P = self.P
        if perm is None:
            perm = self.perm_b[:, 0:128] if d == 128 else self.perm_b[0:64, 128:192]
        sets = self.rope_sets

        def A(tb):
            s_ = sets[tb % 2]
            self.proj_fm(s_["ps_x"][0:d, :], w, wsl, tb, nk=nk, src=src)
            P.copy("dve", s_["xb"][0:d, :], s_["ps_x"][0:d, :])

        def B(tb):
            s_ = sets[tb % 2]
            tsl = slice(tb * 512, (tb + 1) * 512)
            P.matmul(s_["ps_r"][0:d, :], perm, s_["xb"][0:d, :])
            P.tt("dve", s_["t1"][0:d, :], s_["ps_x"][0:d, :], self.cos_t[0:d, tsl], ALU.mult)
            P.tt("dve", s_["t2"][0:d, :], s_["ps_r"][0:d, :], self.sin_t[0:d, tsl], ALU.mult)
            P.tt("pool", dst[0:d, tsl], s_["t1"][0:d, :], s_["t2"][0:d, :], ALU.add)
        A(0); A(1); B(0); A(2); B(1); A(3); B(2); B(3)

    def make_rope_sets(self, st):
        self.rope_sets = [
            dict(xb=self.sb([128, 512], BF16, "r_xb0", st), t1=self.sb([128, 512], F32, "r_t1", st),
                 t2=self.sb([128, 512], F32, "r_t2", st), ps_x=self.pb[0], ps_r=self.pb[2]),
            dict(xb=self.sb([128, 512], BF16, "r_xb1", st), t1=self.ztmp[0], t2=self.ztmp[1],
                 ps_x=self.pb[1], ps_r=self.pb[7])]

    def out_proj_pair(self, yT, w_out_d, pair):
        P = self.P
        for half in range(2):
            w = self.load_w(w_out_d[pair * 256:(pair + 1) * 256, half * 512:(half + 1) * 512]
                            .rearrange("(a p) c -> p a c", p=128), [128, 2, 512])
            for i in range(NT):
                pbk = self.pb[i % 2]
                for j in range(2):
                    P.matmul(pbk, yT[:, j, i * 128:(i + 1) * 128], w[:, j, :], start=(j == 0), stop=(j == 1))
                a = self.acc_t[i][:, half * 512:(half + 1) * 512]
                P.tt("dve", a, a, pbk, ALU.add)

    def layer_norm_out(self, lnw_d, lnb_d, last):
        P = self.P
        st = ExitStack()
        lw = self.sb([128, D], F32, "lnw", st)
        lb = self.sb([128, D], F32, "lnb", st)
        P.dma(lw.ap, lnw_d.partition_broadcast(128)[:, 0, :], w=[lw], dbuf=lw)
        P.dma(lb.ap, lnb_d.partition_broadcast(128)[:, 0, :], w=[lb], dbuf=lb)
        junk = [self.sb([128, D], F32, f"junk{i}", st) for i in range(2)]
        stat = [self.sb([128, 8], F32, f"stat{i}", st) for i in range(2)]
        odone = []
        for i in range(NT):
            a = self.acc_t[i]
            sm = stat[i % 2]
            jk = junk[i % 2]
            P.add("dve", lambda e, sm=sm, a=a: e.reduce_sum(sm.ap[:, 0:1], a.ap, mybir.AxisListType.X), r=[a], w=[sm])
            P.ts("dve", sm[:, 1:2], sm[:, 0:1], -1.0 / D, None, ALU.mult)
            P.ts("dve", a, a, sm[:, 1:2], None, ALU.add)
            P.act(jk, a, AF.Square)
            P.add("dve", lambda e, sm=sm, jk=jk: e.reduce_sum(sm.ap[:, 2:3], jk.ap, mybir.AxisListType.X), r=[jk], w=[sm])
            P.act(sm[:, 3:4], sm[:, 2:3], AF.Sqrt, bias=self.eps_ln, scale=1.0 / D)
            P.recip(sm[:, 4:5], sm[:, 3:4])
            P.stt(a, a, sm[:, 4:5], lw, ALU.mult, ALU.mult)
            P.tt("pool", a, a, lb, ALU.add)
            if last:
                odone.append(P.dma(self.out_d[i * 128:(i + 1) * 128, :], a.ap, r=[a], dbuf=a))
        if last:
            fin = P.add("sp", None)
            P.ops[fin].deps.update(odone)
        P.barrier()
        st.close()

    def attention_head(self, qT, kT, vtok, yT_dst, zs, scale, q2=None, k2=None, mask=None):
        P = self.P
        pts = self.pt_tiles
        LAG = len(pts) - 1
        sb_ps = [self.pb[0], self.pb[1], self.pb[2]] + ([self.pb[7]] if LAG >= 3 else [])
        stms = self.ztmp + ([self.stm_extra] if LAG >= 3 else [])
        steps = []
        for qb in range(4):
            kts = []
            for kt in range(NT):
                if mask is not None:
                    q0, k0 = qb * 512, kt * 128
                    if k0 + 127 < q0 - 1024 or k0 > q0 + 511 + 1024:
                        continue
                kts.append(kt)
            for j, kt in enumerate(kts):
                steps.append((qb, kt, j == 0, j == len(kts) - 1))
        n = len(steps)
        for i in range(n + LAG):
            if i < n:
                qb, kt, first, last = steps[i]
                sps = sb_ps[i % len(sb_ps)]
                qsl = slice(qb * 512, (qb + 1) * 512)
                ksl = slice(kt * 128, (kt + 1) * 128)
                P.matmul(sps, kT[:, ksl], qT[:, qsl], start=True, stop=(q2 is None))
                if q2 is not None:
                    P.matmul(sps, k2[:, ksl], q2[:, qsl], start=False, stop=True)
                pt = pts[i % len(pts)]
                stm = stms[i % len(stms)]
                if mask is not None:
                    o = qb * 512 - kt * 128 + 2048
                    P.stt(stm, sps, scale, mask[:, o:o + 512], ALU.mult, ALU.add)
                    P.act(pt, stm, AF.Exp)
                else:
                    P.copy("dve", stm, sps)
                    P.act(pt, stm, AF.Exp, scale=scale)
            if i >= LAG:
                qb, kt, first, last = steps[i - LAG]
                pt = pts[(i - LAG) % len(pts)]
                ops_ = self.pb[3 + (qb % 2)]
                dps = self.pb[5 + (qb % 2)]
                P.matmul(ops_, vtok[:, kt, :], pt, start=first, stop=last)
                P.matmul(dps, self.ones_b, pt, start=first, stop=last)
                if last:
                    qsl = slice(qb * 512, (qb + 1) * 512)
                    rd = self.att_rd[0]
                    P.copy("dve", rd, dps)
                    P.act(rd, rd, AF.Ln)
                    P.act(rd, rd, AF.Exp, scale=-1.0)
                    P.tt("dve", rd, ops_, rd, ALU.mult)
                    P.tt("pool", yT_dst[:, qsl], rd, zs[:, qsl], ALU.mult)

    def silu_z(self, w_in_d, col, zs):
        P = self.P
        w = self.load_w(w_in_d[:, col:col + 128].rearrange("(k p) c -> p k c", p=128), [128, 8, 128])
        for tb in range(4):
            pbk = self.pb[7] if tb % 2 else self.pb[6]
            self.proj_fm(pbk, w, slice(0, 128), tb)
            zt = self.ztmp[tb % 2]
            P.copy("dve", zt, pbk)
            P.act(zs[:, tb * 512:(tb + 1) * 512], zt, AF.Silu)

    def layer_even(self):
        nc, P = self.nc, self.P
        w_in = self.dram_in("even_w_in", [D, 6208], F32)
        q_norm = self.dram_in("even_q_norm", [1, 768], F32)
        w_uq = self.dram_in("even_w_uq", [768, 1536], F32)
        kv_norm = self.dram_in("even_kv_norm", [1, 256], F32)
        w_ukv = self.dram_in("even_w_ukv", [256, 2048], F32)
        w_out = self.dram_in("even_w_out", [2048, D], F32)
        ln_w = self.dram_in("even_ln_w", [1, D], F32)
        ln_b = self.dram_in("even_ln_b", [1, D], F32)
        c_mask = self.dram_in("c_dilmask", [128, 4096], F32)

        self.load_xT(first=True)
        import os
        stop = int(os.environ.get("KSTOP", "99"))
        if stop == 1:
            self.layer_norm_out(ln_w[0:1, :], ln_b[0:1, :], last=True)
            return

        st = ExitStack()
        def bail():
            P.barrier()
            st.close()
            self.layer_norm_out(ln_w[0:1, :], ln_b[0:1, :], last=True)
        if stop != 22:
            self.rope_tables(0, st)
        if stop == 21:
            return bail()
        qn_w = self.sb([128, 6], F32, "qnw", st)
        kvn_w = self.sb([128, 2], F32, "kvnw", st)
        P.dma(qn_w.ap, q_norm[0, :].rearrange("(c p) -> p c", p=128), w=[qn_w], dbuf=qn_w, allow_slow_non_contiguous=True)
        P.dma(kvn_w.ap, kv_norm[0, :].rearrange("(c p) -> p c", p=128), w=[kvn_w], dbuf=kvn_w, allow_slow_non_contiguous=True)
        if stop == 22:
            return bail()
        cq = self.sb([128, 6, S], BF16, "cq", st)
        ckv = self.sb([128, 2, S], BF16, "ckv", st)
        kr = self.sb([128, S], BF16, "krope", st)
        P.memset("pool", kr[64:128, :], 0.0)
        self.make_rope_sets(st)
        self.pt_tiles = [self.sb([128, 512], BF16, f"pt{i}", st) for i in range(3)]
        self.att_rd = [self.sb([128, 512], F32, f"attrd{i}", st) for i in range(1)]

        st3 = ExitStack()
        sq = [self.sb([128, 512], BF16, f"sq{i}", st3) for i in range(2)]
        rstd = self.sb([128, 512], F32, "rstd", st3)
        def lat(dst, ncks, col0, nw, fdim):
            cnt = 0
            for ck in range(ncks):
                w = self.load_w(w_in[:, col0 + ck * 128: col0 + (ck + 1) * 128].rearrange("(k p) c -> p k c", p=128),
                                [128, 8, 128])
                if stop == 231:
                    continue
                for tb in range(4):
                    pbk = self.pb[cnt % 2]
                    self.proj_fm(pbk, w, slice(0, 128), tb)
                    s_ = sq[cnt % 2]
                    P.copy("dve", dst[:, ck, tb * 512:(tb + 1) * 512], pbk)
                    P.act(s_, dst[:, ck, tb * 512:(tb + 1) * 512], AF.Square)
                    if stop != 232:
                        P.matmul(self.pb[3 + tb], self.ones_b, s_, start=(ck == 0), stop=(ck == ncks - 1))
                    cnt += 1
            if stop in (231, 232, 233):
                return
            for tb in range(4):
                P.copy("dve", rstd, self.pb[3 + tb])
                P.act(rstd, rstd, AF.Ln, bias=self.eps_rms, scale=1.0 / fdim)
                P.act(rstd, rstd, AF.Exp, scale=-0.5)
                for ck in range(ncks):
                    dv = dst[:, ck, tb * 512:(tb + 1) * 512]
                    P.stt(dv, dv, nw[:, ck:ck + 1], rstd, ALU.mult, ALU.mult)

        lat(cq, 6, 0, qn_w, 768.0)
        lat(ckv, 2, 768, kvn_w, 256.0)
        P.barrier()
        st3.close()
        if stop in (23, 231, 232, 233):
            return bail()
        w = self.load_w(w_in[:, 1024:1088].rearrange("(k p) c -> p k c", p=128), [128, 8, 64])
        self.rope_proj4(kr, w, slice(0, 64), 8, None, 64)

        if stop == 2:
            P.barrier()
            st.close()
            self.layer_norm_out(ln_w[0:1, :], ln_b[0:1, :], last=True)
            return
        qn = self.sb([128, S], BF16, "qn", st)
        qr = self.sb([128, S], BF16, "qr", st)
        P.memset("pool", qr[64:128, :], 0.0)
        kn = self.sb([128, S], BF16, "kn", st)
        vt = self.sb([128, NT, 128], BF16, "vt", st)
        zs = self.sb([128, S], BF16, "zs", st)
        yT = self.sb([128, 2, S], BF16, "yT", st)
        zoff = 768 + 256 + 64 + 3 * 1024
        scale_mla = 192.0 ** -0.5
        for h in range(8):
            wq = self.load_w(w_uq[:, h * 192:(h + 1) * 192].rearrange("(k p) c -> p k c", p=128), [128, 6, 192])
            for tb in range(4):
                pbk = self.pb[3 + tb % 2]
                self.proj_fm(pbk, wq, slice(0, 128), tb, nk=6, src=cq)
                P.copy("dve", qn[:, tb * 512:(tb + 1) * 512], pbk)
            self.rope_proj4(qr, wq, slice(128, 192), 6, cq, 64)
            wkv = self.load_w(w_ukv[:, h * 256:(h + 1) * 256].rearrange("(k p) c -> p k c", p=128), [128, 2, 256])
            for tb in range(4):
                pbk = self.pb[tb % 2]
                self.proj_fm(pbk, wkv, slice(0, 128), tb, nk=2, src=ckv)
                P.copy("dve", kn[:, tb * 512:(tb + 1) * 512], pbk)
            for g in range(4):
                pbk = self.pb[g % 2]
                for j in range(4):
                    i = g * 4 + j
                    for k in range(2):
                        P.matmul(pbk[:, j * 128:(j + 1) * 128], ckv[:, k, i * 128:(i + 1) * 128], wkv[:, k, 128:256],
                                 start=(k == 0), stop=(k == 1))
                P.copy("dve", V(vt, vt.ap[:, g * 4:(g + 1) * 4, :]),
                       V(pbk, pbk.ap.rearrange("p (a b) -> p a b", a=4)))
            self.silu_z(w_in, zoff + h * 128, zs)
            self.attention_head(qn, kn, vt, yT[:, h % 2, :], zs, scale_mla, q2=qr, k2=kr)
            if h % 2 == 1:
                self.out_proj_pair(yT, w_out, h // 2)
            if stop == 3 and h == 1:
                break
        P.barrier()
        st.close()
        if stop == 3:
            self.layer_norm_out(ln_w[0:1, :], ln_b[0:1, :], last=True)
            return

        st = ExitStack()
        self.rope_tables(1, st)
        mask = self.sb([128, 4096], F32, "mask", st)
        P.dma(mask.ap, c_mask, w=[mask], dbuf=mask)
        self.make_rope_sets(st)
        self.pt_tiles = [self.sb([128, 512], BF16, f"pt{i}", st) for i in range(4)]
        self.stm_extra = self.sb([128, 512], F32, "stm_x", st)
        self.att_rd = [self.sb([128, 512], F32, f"attrd{i}", st) for i in range(1)]
        qn = self.sb([128, S], BF16, "qn", st)
        kn = self.sb([128, S], BF16, "kn", st)
        vt = self.sb([128, NT, 128], BF16, "vt", st)
        zs = self.sb([128, S], BF16, "zs", st)
        yT = self.sb([128, 2, S], BF16, "yT", st)
        qoff, koff, voff = 1088, 1088 + 1024, 1088 + 2048
        scale_d = 128.0 ** -0.5
        for h in range(8):
            for (dst, off) in ((qn, qoff), (kn, koff)):
                w = self.load_w(w_in[:, off + h * 128: off + (h + 1) * 128].rearrange("(k p) c -> p k c", p=128),
                                [128, 8, 128])
                self.rope_proj4(dst, w, slice(0, 128), 8, None, 128)
            w = self.load_w(w_in[:, voff + h * 128: voff + (h + 1) * 128].rearrange("(k p) c -> p k c", p=128),
                            [128, 8, 128])
            for g in range(4):
                pbk = self.pb[g % 2]
                for j in range(4):
                    i = g * 4 + j
                    for k in range(8):
                        P.matmul(pbk[:, j * 128:(j + 1) * 128],
                                 self.xT_tb[i // 4][:, k, (i % 4) * 128:(i % 4 + 1) * 128], w[:, k, :],
                                 start=(k == 0), stop=(k == 7))
                P.copy("dve", V(vt, vt.ap[:, g * 4:(g + 1) * 4, :]),
                       V(pbk, pbk.ap.rearrange("p (a b) -> p a b", a=4)))
            self.silu_z(w_in, zoff + 1024 + h * 128, zs)
            self.attention_head(qn, kn, vt, yT[:, h % 2, :], zs, scale_d, mask=mask)
            if h % 2 == 1:
                self.out_proj_pair(yT, w_out, 4 + h // 2)
        P.barrier()
        st.close()
        self.layer_norm_out(ln_w[0:1, :], ln_b[0:1, :], last=(self.debug_layers == 1))

    def consts_small(self):
        P = self.P
        self.cst = self.sb([128, 8], F32, "cst")
        self.mpi = self.cst[:, 0:1]
        self.eps_rms = self.cst[:, 1:2]
        self.eps_ln = self.cst[:, 2:3]
        P.memset("pool", self.cst[:, 0:1], -PI * (1.0 - 1e-6))
        P.memset("pool", self.cst[:, 1:2], 1e-6)
        P.memset("pool", self.cst[:, 2:3], 1e-5)
        self.one_c = self.cst[:, 3:4]
        P.memset("pool", self.cst[:, 3:4], 1.0)

    def build(self):
        self.setup_common()
        self.consts_small()
        self.layer_even()
        if self.debug_layers >= 2:
            self.layer_odd()
        self.P.emit(self.nc)
        self.es.close()


    def layer_odd(self):
        nc, P = self.nc, self.P
        w_in = self.dram_in("odd_w_in", [D, 7200], F32)
        conv_w = self.dram_in("odd_conv_w", [5, 3072], F32)
        a_log = self.dram_in("odd_a_log", [2, 8], F32)
        dt_bias = self.dram_in("odd_dt_bias", [2, 8], F32)
        gdn_norm = self.dram_in("odd_gdn_norm", [1, 128], F32)
        rn_w = self.dram_in("odd_ret_norm_w", [1, 1024], F32)
        rn_b = self.dram_in("odd_ret_norm_b", [1, 1024], F32)
        w_out = self.dram_in("odd_w_out", [2048, D], F32)
        ln_w = self.dram_in("odd_ln_w", [1, D], F32)
        ln_b = self.dram_in("odd_ln_b", [1, D], F32)
        c_gm = self.dram_in("c_gmask", [128, 1024], F32)
        c_perm2 = self.dram_in("c_perm2", [128, 128], F32)
        c_dec = self.dram_in("c_retdecay", [8, 128, 4096], F32)
        X = mybir.AxisListType.X

        self.load_xT(first=False)
        zoff = 5152

        st = ExitStack()
        gm = self.sb([128, 1024], F32, "gm", st)
        P.dma(gm.ap, c_gm, w=[gm], dbuf=gm)
        trif, trib = gm[:, 0:128], gm[:, 128:256]
        mSL, mSU, mLI, mUI, nSL, nSU = (gm[:, (2 + i) * 128:(3 + i) * 128] for i in range(6))
        ones_f = self.sb([128, 128], F32, "onesf", st)
        P.memset("pool", ones_f, 1.0)
        gnw = self.sb([128, 1], F32, "gnw", st)
        P.dma(gnw.ap, gdn_norm[0, :].rearrange("(p o) -> p o", o=1), w=[gnw], dbuf=gnw)
        cols = {nm: self.sb([128, NT, 16], F32, "c_" + nm, st)
                for nm in ("beta", "nbeta", "gc", "ec", "kd", "wc", "glx")}
        st0 = ExitStack()
        bat = self.sb([128, NT, 32], F32, "bat", st0)
        gtk = self.sb([128, NT, 16], F32, "gtk", st0)
        gtot = self.sb([128, NT, 16], F32, "gtot", st0)
        alb = self.sb([128, 16], F32, "alb", st0)
        dtb = self.sb([128, 16], F32, "dtb", st0)
        P.dma(alb.ap, a_log.rearrange("a b -> (a b)").partition_broadcast(128), w=[alb], dbuf=alb)
        P.dma(dtb.ap, dt_bias.rearrange("a b -> (a b)").partition_broadcast(128), w=[dtb], dbuf=dtb)
        wba = self.load_w(w_in[:, 3072:3104].rearrange("(k p) c -> p k c", p=128), [128, 8, 32])
        pbk = self.pb[0]
        for t in range(NT):
            for k in range(8):
                P.matmul(pbk[:, t * 32:(t + 1) * 32], self.xT_tb[t // 4][:, k, (t % 4) * 128:(t % 4 + 1) * 128],
                         wba[:, k, :], start=(k == 0), stop=(k == 7))
        P.copy("dve", V(bat, bat.ap.rearrange("p t e -> p (t e)")), pbk)
        P.act(cols["beta"], bat[:, :, 0:16], AF.Sigmoid)
        P.ts("dve", cols["nbeta"], cols["beta"], -1.0, None, ALU.mult)
        P.act(alb, alb, AF.Exp)
        P.ts("dve", alb, alb, -1.0, None, ALU.mult)
        for t in range(NT):
            P.tt("dve", gtk[:, t, :], bat[:, t, 16:32], dtb, ALU.add)
        P.act(gtk, gtk, AF.Exp)
        P.act(gtk, gtk, AF.Ln, bias=self.one_c)
        for t in range(NT):
            P.tt("dve", gtk[:, t, :], gtk[:, t, :], alb, ALU.mult)
        pb1, pb2 = self.pb[1], self.pb[2]
        for t in range(NT):
            P.matmul(pb1[:, t * 16:t * 16 + 8], trif, gtk[:, t, 0:8])
            P.matmul(pb1[:, t * 16 + 8:t * 16 + 16], trib, gtk[:, t, 8:16])
            P.matmul(pb2[:, t * 16:(t + 1) * 16], ones_f, gtk[:, t, :])
        P.copy("dve", V(cols["gc"], cols["gc"].ap.rearrange("p t e -> p (t e)")), pb1[:, 0:256])
        P.copy("dve", V(gtot, gtot.ap.rearrange("p t e -> p (t e)")), pb2[:, 0:256])
        P.act(cols["ec"], cols["gc"], AF.Exp)
        P.tt("dve", cols["wc"], cols["ec"], cols["beta"], ALU.mult)
        P.act(cols["glx"], gtot, AF.Exp)
        P.tt("dve", gtot, gtot, cols["gc"], ALU.subtract)
        P.act(cols["kd"], gtot, AF.Exp)
        P.barrier()
        st0.close()

        pre = self.sb([128, S + 4], BF16, "pre", st)
        P.memset("pool", pre, 0.0)
        qT = self.sb([128, S], BF16, "gqT", st)
        kT = self.sb([128, S], BF16, "gkT", st)
        ktok = self.sb([128, NT, 128], BF16, "ktok", st)
        vtok = self.sb([128, NT, 128], BF16, "vtok", st)
        cw = self.sb([128, 5], F32, "cw", st)
        dgm = self.sb([128, 5, 128], BF16, "dgm", st)
        oT = self.sb([128, S], F32, "goT", st)
        oT_b = [T(oT.ap[:, i * 512:(i + 1) * 512], f"oTb{i}") for i in range(4)]
        oT_t = [V(oT_b[i // 4], oT.ap[:, i * 128:(i + 1) * 128]) for i in range(NT)]
        zs = self.sb([128, S], BF16, "zs", st)
        vT = zs
        yT = self.sb([128, 2, S], BF16, "yT", st)
        sqb = self.sb([128, 512], BF16, "sqb", st)
        sqb2 = [sqb, V(pre, pre.ap[:, 1028:1540])]
        rst = self.sb([128, 512], F32, "rst", st)
        rst2 = V(pre, pre.ap[:, 4:1028].bitcast(F32))
        Sf = [self.sb([128, 128], F32, f"Sf{c}", st) for c in range(2)]
        Sb = [self.sb([128, 128], BF16, f"Sb{c}", st) for c in range(2)]
        NL = 4

        def f32t(nm):
            return self.sb([128, 128], F32, nm, st)

        def b16t(nm):
            return self.sb([128, 128], BF16, nm, st)
        lanes = []
        for l in range(NL):
            ln = dict(A=f32t(f"lA{l}"), B=f32t(f"lB{l}"), C=f32t(f"lC{l}"), H=f32t(f"lH{l}"), Pt=f32t(f"lP{l}"),
                      XX=[self.sb([128, 256], F32, f"lX{l}{i}", st) for i in range(2)],
                      Ptb=b16t(f"lPb{l}"), kbg=b16t(f"lkb{l}"), vb=b16t(f"lvb{l}"), bank=self.pb[l],
                      out=[dict(wTs=b16t(f"owT{l}{r}"), us=f32t(f"ous{l}{r}"), qg=b16t(f"oqg{l}{r}"),
                                attnT=b16t(f"oat{l}{r}"), kdt=b16t(f"okd{l}{r}")) for r in range(2)])
            lanes.append(ln)
        cbank = [self.pb[4], self.pb[5]]

        def conv_chunk(col, dst, normalize, post_scale):
            w = self.load_w(w_in[:, col:col + 128].rearrange("(k p) c -> p k c", p=128), [128, 8, 128])
            P.dma(cw.ap, conv_w[:, col:col + 128].rearrange("j c -> c j"), w=[cw], dbuf=cw,
                  allow_slow_non_contiguous=True)
            for j in range(5):
                P.ts("dve", dgm[:, j, :], self.ident_b, cw[:, j:j + 1], None, ALU.mult)
            for tb in range(4):
                pk = self.pb[6 + tb % 2]
                self.proj_fm(pk, w, slice(0, 128), tb)
                P.copy("dve", pre[:, 2 + tb * 512: 2 + (tb + 1) * 512], pk)
            for tb in range(4):
                pk = self.pb[6 + tb % 2]
                for j in range(5):
                    P.matmul(pk, dgm[:, j, :], pre[:, tb * 512 + j: tb * 512 + j + 512], start=(j == 0), stop=(j == 4))
                zt = self.ztmp[tb % 2]
                P.copy("dve", zt, pk)
                P.act(dst[:, tb * 512:(tb + 1) * 512], zt, AF.Silu)
            if normalize:
                rl = [rst, rst2, self.ztmp[0], self.ztmp[1]]
                sl4 = [slice(tb * 512, (tb + 1) * 512) for tb in range(4)]
                for tb in range(4):
                    pk = self.pb[6 + tb % 2]
                    P.act(sqb2[tb % 2], dst[:, sl4[tb]], AF.Square)
                    P.matmul(pk, self.ones_b, sqb2[tb % 2])
                    P.copy("dve", rl[tb], pk)
                for tb in range(4):
                    P.act(rl[tb], rl[tb], AF.Ln, bias=self.eps_rms, scale=1.0)
                for tb in range(4):
                    P.act(rl[tb], rl[tb], AF.Exp, scale=-0.5)
                for tb in range(4):
                    P.stt(dst[:, sl4[tb]], dst[:, sl4[tb]], post_scale, rl[tb], ALU.mult, ALU.mult)

        def prep_gen(l, dr, t, r, h):
            ln = lanes[l]
            o = ln["out"][r]
            bk = ln["bank"]
            e = dr * 8 + h
            mN, mA = (mSL, mUI) if dr == 0 else (mSU, mLI)
            tsl = slice(t * 128, (t + 1) * 128)
            c_ = lambda nm: cols[nm][:, t, e:e + 1]
            A, B, C, H, Pt, XX = ln["A"], ln["B"], ln["C"], ln["H"], ln["Pt"], ln["XX"]
            P.ts("pool", A, self.ident_f, c_("gc"), None, ALU.mult)
            P.ts("pool", ln["kbg"], ktok[:, t, :], c_("wc"), None, ALU.mult)
            P.ts("pool", ln["vb"], vtok[:, t, :], c_("beta"), None, ALU.mult)
            P.ts("pool", o["kdt"], ktok[:, t, :], c_("kd"), None, ALU.mult)
            P.matmul(bk[:, 0:128], ones_f, A)
            P.matmul(bk[:, 256:384], kT[:, tsl], kT[:, tsl])
            P.matmul(bk[:, 384:512], kT[:, tsl], qT[:, tsl])
            yield
            P.ts("dve", H, bk[:, 0:128], c_("gc"), None, ALU.subtract)
            P.ts("pool", A, H, 3.0e38, 0.0, ALU.min, ALU.max)
            P.tt("pool", B, H, A, ALU.subtract)
            P.act(A, A, AF.Exp, scale=-1.0)
            P.act(B, B, AF.Exp)
            P.act(H, H, AF.Exp, bias=c_("gc"))
            yield
            P.tt("dve", C, bk[:, 256:384], A, ALU.mult)
            P.stt(XX[0][:, 0:128], C, c_("nbeta"), mN, ALU.mult, ALU.mult)
            P.matmul(bk[:, 128:256], XX[0][:, 0:128], self.ident_f)
            P.tt("dve", C, bk[:, 384:512], B, ALU.mult)
            P.tt("pool", o["attnT"], C, mA, ALU.mult)
            P.tt("pool", o["qg"], qT[:, tsl], H, ALU.mult)
            yield
            P.copy("dve", XX[0][:, 128:256], bk[:, 128:256])
            P.tt("pool", Pt, XX[0][:, 128:256], self.ident_f, ALU.add)
            cur = XX[0]
            for k in range(1, 7):
                nxt = XX[k % 2]
                P.matmul(bk[:, 0:128], cur[:, 128:256], cur[:, 0:128])
                if k < 6:
                    P.matmul(bk[:, 128:256], cur[:, 0:128], cur[:, 128:256])
                yield
                if k < 6:
                    P.copy("dve", nxt, bk[:, 0:256])
                else:
                    P.copy("dve", nxt[:, 0:128], bk[:, 0:128])
                P.matmul(bk[:, 256:384], nxt[:, 0:128], Pt)
                cur = nxt
                yield
                P.tt("dve", Pt, Pt, bk[:, 256:384], ALU.add)
            P.copy("pool", ln["Ptb"], Pt)
            P.matmul(bk[:, 384:512], ln["kbg"], ln["Ptb"])
            P.matmul(bk[:, 256:384], ln["Ptb"], ln["vb"])
            yield
            P.copy("dve", o["wTs"], bk[:, 384:512])
            P.copy("dve", o["us"], bk[:, 256:384])
            yield

        def rec_gen(items, h):
            by_chain = {0: [it for it in items if it[0] == 0], 1: [it for it in items if it[0] == 1]}
            nstep = max(len(v) for v in by_chain.values())
            for sidx in range(nstep):
                cur = [by_chain[c][sidx] for c in (0, 1) if sidx < len(by_chain[c])]
                for (c, l, r, t) in cur:
                    o = lanes[l]["out"][r]
                    P.matmul(cbank[c][:, 0:128], o["wTs"], Sb[c])
                yield
                vn = {}
                for (c, l, r, t) in cur:
                    o = lanes[l]["out"][r]
                    vnb = lanes[l]["C"]
                for (c, l, r, t) in cur:
                    o = lanes[l]["out"][r]
                    P.tt("dve", vnbs[c], o["us"], cbank[c][:, 0:128], ALU.subtract)
                    P.matmul(cbank[c][:, 128:256], Sb[c], o["qg"], start=True, stop=False)
                    P.matmul(cbank[c][:, 128:256], vnbs[c], o["attnT"], start=False, stop=True)
                    P.matmul(cbank[c][:, 256:384], o["kdt"], vnbs[c])
                yield
                for (c, l, r, t) in cur:
                    e = c * 8 + h
                    first = (c == 0 and t <= 7) or (c == 1 and t >= 8)
                    if first:
                        P.copy("dve", oT_t[t], cbank[c][:, 128:256])
                    else:
                        P.tt("dve", oT_t[t], oT_t[t], cbank[c][:, 128:256], ALU.add)
                    P.stt(Sf[c], Sf[c], cols["glx"][:, t, e:e + 1], cbank[c][:, 256:384], ALU.mult, ALU.add)
                    P.copy("pool", Sb[c], Sf[c])
                yield

        vnbs = [b16t(f"vnb{c}") for c in range(2)]

        def run_interleaved(gens):
            gens = [g for g in gens if g is not None]
            while gens:
                alive = []
                for g in gens:
                    try:
                        next(g)
                        alive.append(g)
                    except StopIteration:
                        pass
                gens = alive

        for h in range(8):
            conv_chunk(h * 128, qT, True, 128.0 ** -0.5)
            conv_chunk(1024 + h * 128, kT, True, 1.0)
            conv_chunk(2048 + h * 128, vT, False, 1.0)
            for (srcT, dtok) in ((kT, ktok), (vT, vtok)):
                for g in range(4):
                    pk = self.pb[6 + g % 2]
                    for j in range(4):
                        i = g * 4 + j
                        P.matmul(pk[:, j * 128:(j + 1) * 128], srcT[:, i * 128:(i + 1) * 128], self.ident_b)
                    P.copy("dve", V(dtok, dtok.ap[:, g * 4:(g + 1) * 4, :]), V(pk, pk.ap.rearrange("p (a b) -> p a b", a=4)))
            self.silu_z(w_in, zoff + h * 128, zs)
            for c in range(2):
                P.memset("pool", Sf[c], 0.0)
                P.memset("pool", Sb[c], 0.0)
            ngroups = NT // 2
            prev_rec = None
            for g in range(ngroups + 1):
                gens = []
                if g < ngroups:
                    r = g % 2
                    work = [(0, 0, 2 * g), (1, 1, 15 - 2 * g), (2, 0, 2 * g + 1), (3, 1, 14 - 2 * g)]
                    gens += [prep_gen(l, dr, t, r, h) for (l, dr, t) in work]
                if prev_rec is not None:
                    gens.append(rec_gen(prev_rec, h))
                run_interleaved(gens)
                if g < ngroups:
                    prev_rec = [(dr, l, g % 2, t) for (l, dr, t) in work]
            rl = [rst, rst2, self.ztmp[0], self.ztmp[1]]
            sl4 = [slice(tb * 512, (tb + 1) * 512) for tb in range(4)]
            for tb in range(4):
                pk = self.pb[6 + tb % 2]
                P.act(sqb2[tb % 2], oT_b[tb], AF.Square)
                P.matmul(pk, self.ones_b, sqb2[tb % 2])
                P.copy("dve", rl[tb], pk)
            for tb in range(4):
                P.act(rl[tb], rl[tb], AF.Ln, bias=self.eps_rms, scale=1.0 / 128.0)
            for tb in range(4):
                P.act(rl[tb], rl[tb], AF.Exp, scale=-0.5)
            for tb in range(4):
                P.stt(rl[tb], oT_b[tb], gnw[:, 0:1], rl[tb], ALU.mult, ALU.mult)
                P.tt("pool", yT[:, h % 2, sl4[tb]], rl[tb], zs[:, sl4[tb]], ALU.mult)
            if h % 2 == 1:
                self.out_proj_pair(yT, w_out, h // 2)
        P.barrier()
        st.close()

        st = ExitStack()
        self.rope_tables(0, st)
        perm2f = self.sb([128, 128], F32, "perm2f", st)
        perm2 = self.sb([128, 128], BF16, "perm2", st)
        P.dma(perm2f.ap, c_perm2, w=[perm2f], dbuf=perm2f)
        P.copy("dve", perm2, perm2f)
        ones_f = self.sb([128, 128], F32, "onesf", st)
        P.memset("pool", ones_f, 1.0)
        rnw = self.sb([128, 8], F32, "rnw", st)
        rnb = self.sb([128, 8], F32, "rnb", st)
        P.dma(rnw.ap, rn_w[0, :].rearrange("(c p) -> p c", p=128), w=[rnw], dbuf=rnw, allow_slow_non_contiguous=True)
        P.dma(rnb.ap, rn_b[0, :].rearrange("(c p) -> p c", p=128), w=[rnb], dbuf=rnb, allow_slow_non_contiguous=True)
        self.make_rope_sets(st)
        qT = self.sb([128, S], BF16, "rqT", st)
        kT = self.sb([128, S], BF16, "rkT", st)
        kTa = self.sb([128, S], BF16, "rkTa", st)
        kTb = self.sb([128, S], BF16, "rkTb", st)
        P.memset("pool", kTa[64:128, :], 0.0)
        P.memset("pool", kTb[0:64, :], 0.0)
        vt = self.sb([128, NT, 128], BF16, "rvt", st)
        dec = self.sb([128, 4096], BF16, "dec", st)
        pts = [self.sb([128, 512], BF16, f"rpt{i}", st) for i in range(4)]
        oR = self.sb([128, S], F32, "oR", st)
        cen4 = [self.sb([128, 512], F32, f"cen{i}", st) for i in range(4)]
        sqf2 = [self.sb([128, 512], F32, f"sqf{i}", st) for i in range(2)]
        zs = self.sb([128, S], BF16, "zs", st)
        yT = self.sb([128, 2, S], BF16, "yT", st)
        qoff, koff, voff = 3104, 3616, 4128
        for h in range(8):
            if h % 2 == 0:
                for (dst, off) in ((qT, qoff), (kT, koff)):
                    w = self.load_w(w_in[:, off + (h // 2) * 128: off + (h // 2 + 1) * 128]
                                    .rearrange("(k p) c -> p k c", p=128), [128, 8, 128])
                    self.rope_proj4(dst, w, slice(0, 128), 8, None, 128, perm=perm2)
                P.copy("pool", kTa[0:64, :], kT[0:64, :])
                P.copy("pool", kTb[64:128, :], kT[64:128, :])
            w = self.load_w(w_in[:, voff + h * 128: voff + (h + 1) * 128].rearrange("(k p) c -> p k c", p=128),
                            [128, 8, 128])
            for g in range(4):
                pbk = self.pb[g % 2]
                for j in range(4):
                    i = g * 4 + j
                    for k in range(8):
                        P.matmul(pbk[:, j * 128:(j + 1) * 128],
                                 self.xT_tb[i // 4][:, k, (i % 4) * 128:(i % 4 + 1) * 128], w[:, k, :],
                                 start=(k == 0), stop=(k == 7))
                P.copy("dve", V(vt, vt.ap[:, g * 4:(g + 1) * 4, :]), V(pbk, pbk.ap.rearrange("p (a b) -> p a b", a=4)))
            for c in range(4):
                stg = self.wst[self.wst_i % 2]
                self.wst_i += 1
                P.dma(stg.ap[:, 0:1024], c_dec[h, :, c * 1024:(c + 1) * 1024], w=[stg], dbuf=stg)
                P.copy("pool", dec[:, c * 1024:(c + 1) * 1024], stg[:, 0:1024])
            self.silu_z(w_in, zoff + 1024 + h * 128, zs)
            hp = (h % 2) * 64
            steps = [(qb, kt) for qb in range(4) for kt in range(NT)]
            LAG = 3
            sbk = [self.pb[0], self.pb[1], self.pb[2], self.pb[7]]
            for i in range(len(steps) + LAG):
                if i < len(steps):
                    qb, kt = steps[i]
                    qsl = slice(qb * 512, (qb + 1) * 512)
                    ksl = slice(kt * 128, (kt + 1) * 128)
                    sps = sbk[i % 4]
                    pt = pts[i % 4]
                    P.matmul(sps, (kTa if h % 2 == 0 else kTb)[:, ksl], qT[:, qsl])
                    o = qb * 512 - kt * 128 + 2048
                    P.stt(pt, sps, 0.125, dec[:, o:o + 512], ALU.mult, ALU.mult)
                if i >= LAG:
                    qb, kt = steps[i - LAG]
                    pacc = self.pb[3 + (qb % 2)]
                    P.matmul(pacc, vt[:, kt, :], pts[(i - LAG) % 4], start=(kt == 0), stop=(kt == NT - 1))
                    if kt == NT - 1:
                        P.copy("dve", oR[:, qb * 512:(qb + 1) * 512], pacc)
            sl4 = [slice(tb * 512, (tb + 1) * 512) for tb in range(4)]
            rl = [sqf2[0], sqf2[1], self.ztmp[0], self.ztmp[1]]
            for tb in range(4):
                pk = self.pb[5 + tb % 2]
                P.matmul(pk, ones_f, oR[:, sl4[tb]])
                P.stt(cen4[tb], pk, -1.0 / 128.0, oR[:, sl4[tb]], ALU.mult, ALU.add)
            for tb in range(4):
                P.tt("pool", rl[tb], cen4[tb], cen4[tb], ALU.mult)
            for tb in range(4):
                pk2 = self.pb[5 + tb % 2]
                P.matmul(pk2, ones_f, rl[tb])
                P.copy("dve", rl[tb], pk2)
            for tb in range(4):
                P.act(rl[tb], rl[tb], AF.Ln, bias=self.eps_ln, scale=1.0 / 128.0)
            for tb in range(4):
                P.act(rl[tb], rl[tb], AF.Exp, scale=-0.5)
            for tb in range(4):
                P.tt("dve", cen4[tb], cen4[tb], rl[tb], ALU.mult)
                P.ts("dve", cen4[tb], cen4[tb], rnw[:, h:h + 1], rnb[:, h:h + 1], ALU.mult, ALU.add)
                P.tt("pool", yT[:, h % 2, sl4[tb]], cen4[tb], zs[:, sl4[tb]], ALU.mult)
            if h % 2 == 1:
                self.out_proj_pair(yT, w_out, 4 + h // 2)
        P.barrier()
        st.close()
        self.layer_norm_out(ln_w[0:1, :], ln_b[0:1, :], last=True)


def host_consts():
    c = {}
    c["c_ident"] = np.eye(128, dtype=np.float32)
    invf = np.zeros((128, 2), np.float32)
    sgn = np.zeros((128, 2), np.float32)
    f64 = (np.float32(THETA) ** (-np.arange(0, 64, 2, dtype=np.float32) / np.float32(64))).astype(np.float32)
    f128 = (np.float32(THETA) ** (-np.arange(0, 128, 2, dtype=np.float32) / np.float32(128))).astype(np.float32)
    for p in range(128):
        invf[p, 0] = f64[p % 32]
        sgn[p, 0] = -1.0 if (p % 64) < 32 else 1.0
        invf[p, 1] = f128[p % 64]
        sgn[p, 1] = -1.0 if p < 64 else 1.0
    c["c_invf"] = invf
    c["c_sgn"] = sgn
    perm = np.zeros((128, 256), np.float32)
    for i in range(128):
        perm[(i + 64) % 128, i] = 1.0
    for i in range(64):
        perm[(i + 32) % 64, 128 + i] = 1.0
    c["c_perm"] = perm
    dd = np.arange(4096)[None, :] - np.arange(128)[:, None] - 2048
    ad = np.abs(dd)
    m = (ad <= 64).astype(np.float32) + ((dd % 4 == 0) & (ad <= 256)).astype(np.float32) \
        + ((dd % 16 == 0) & (ad <= 1024)).astype(np.float32)
    with np.errstate(divide="ignore"):
        c["c_dilmask"] = np.where(m > 0, np.log(np.maximum(m, 1.0)), -100.0).astype(np.float32)
    i_ = np.arange(128)[:, None]
    j_ = np.arange(128)[None, :]
    trif = (i_ <= j_).astype(np.float32)
    trib = (i_ >= j_).astype(np.float32)
    SL = (i_ > j_).astype(np.float32)
    SU = (i_ < j_).astype(np.float32)
    LI = (i_ >= j_).astype(np.float32)
    UI = (i_ <= j_).astype(np.float32)
    c["c_gmask"] = np.concatenate([trif, trib, SL, SU, LI, UI, -SL, -SU], axis=1).astype(np.float32)
    p2 = np.zeros((128, 128), np.float32)
    for blk in range(2):
        for i in range(64):
            p2[blk * 64 + (i + 32) % 64, blk * 64 + i] = 1.0
    c["c_perm2"] = p2
    dd = np.abs(np.arange(4096)[None, :] - np.arange(128)[:, None] - 2048).astype(np.float32)
    lg = np.log1p(-(np.float32(2.0) ** (-5.0 - np.arange(8, dtype=np.float32)))).astype(np.float32)
    c["c_retdecay"] = np.exp(lg[:, None, None] * dd[None]).astype(np.float32)
    return c


_CACHE = {}


def get_nc(debug_layers=2):
    if debug_layers not in _CACHE:
        nc = bass.Bass("TRN2", target_bir_lowering=False)
        b = Builder(nc, debug_layers)
        b.build()
        _CACHE[debug_layers] = (nc, b)
    return _CACHE[debug_layers][0]


def kernel(debug_layers=2, **inputs):
    import os
    debug_layers = int(os.environ.get("KLAYERS", debug_layers))
    nc = get_nc(debug_layers)
    consts = host_consts()
    x = np.ascontiguousarray(inputs["x"], dtype=np.float32)
    pos = np.ascontiguousarray(inputs["positions"]).astype(np.int32)
    shared = dict(consts)
    if debug_layers == 1:
        for k in ("c_gmask", "c_perm2", "c_retdecay"):
            shared.pop(k)
    for k, v in inputs.items():
        if k in ("x", "positions"):
            continue
        if debug_layers == 1 and k.startswith("odd_"):
            continue
        a = np.ascontiguousarray(v[0], dtype=np.float32)
        if a.ndim == 1:
            a = a[None, :]
        shared[k] = a
    in_maps = []
    for b in range(8):
        m = dict(shared)
        m["x"] = x[b]
        m["pos"] = pos[b:b + 1]
        in_maps.append(m)
    res = run_bass_kernel_spmd(nc, in_maps, core_ids=list(range(8)))
    return np.stack([r["out"] for r in res.results], axis=0).astype(np.float32)
```

```python
import math
from contextlib import ExitStack
import numpy as np
import concourse.bass as bass
import concourse.mybir as mybir
from concourse.bass_utils import run_bass_kernel_spmd

F32 = mybir.dt.float32
BF16 = mybir.dt.bfloat16
I32 = mybir.dt.int32
ALU = mybir.AluOpType
AF = mybir.ActivationFunctionType

S = 2048
D = 1024
NT = 16
ALPHA = 4.0 ** 0.25
THETA = 10000.0
PI = math.pi


class V:
    def __init__(self, t, ap):
        self.t = t
        self.ap = ap

    def __getitem__(self, k):
        return V(self.t, self.ap[k])


class T(V):
    def __init__(self, ap, name=""):
        self.t = self
        self.ap = ap
        self.name = name
        self.w = None
        self.rd = {}
        self.rd_dma = []
        self.dma_ops = []
        self.sem = None


class Op:
    __slots__ = ("eng", "fn", "deps", "signal", "cnt", "is_dma", "dbuf", "dval", "group")

    def __init__(self, eng, fn):
        self.eng = eng
        self.fn = fn
        self.deps = set()
        self.signal = False
        self.cnt = None
        self.is_dma = False
        self.dbuf = None
        self.dval = 0
        self.group = None


SEG = 8000


class Prog:
    ENGS = ("pe", "act", "dve", "pool", "sp")

    def __init__(self):
        self.ops = []
        self.last = {e: None for e in self.ENGS}
        self.groups = {}

    def add(self, eng, fn, r=(), w=(), dbuf=None, group=None):
        i = len(self.ops)
        op = Op(eng, fn)
        is_dma = dbuf is not None or group is not None
        op.is_dma = is_dma
        deps = op.deps
        for v in r:
            t = v.t
            if t.w is not None:
                deps.add(t.w)
        for v in w:
            t = v.t
            if t.w is not None:
                deps.add(t.w)
            deps.update(t.rd.values())
            deps.update(t.rd_dma)
        if dbuf is not None:
            if dbuf.dma_ops:
                deps.add(dbuf.dma_ops[-1])
            dbuf.dma_ops.append(i)
            op.dbuf = dbuf
            op.dval = 16 * len(dbuf.dma_ops)
        if group is not None:
            g = self.groups.setdefault(group, [])
            g.append(i)
            op.group = group
        for v in r:
            t = v.t
            if is_dma:
                t.rd_dma.append(i)
            else:
                t.rd[eng] = i
        for v in w:
            t = v.t
            t.w = i
            t.rd = {}
            t.rd_dma = []
        self.ops.append(op)
        if fn is not None:
            self.last[eng] = i
        return i

    def barrier(self):
        lasts = [v for v in self.last.values() if v is not None]
        for e in ("pe", "act", "dve", "pool", "sp"):
            i = self.add(e, None)
            self.ops[i].deps.update(x for x in lasts)

    def matmul(self, out, lhsT, rhs, start=True, stop=True, **kw):
        return self.add("pe", lambda e: e.matmul(out.ap, lhsT.ap, rhs.ap, start=start, stop=stop, **kw),
                        r=[lhsT, rhs], w=[out])

    def transpose(self, out, in_, ident):
        return self.add("pe", lambda e: e.transpose(out.ap, in_.ap, ident.ap), r=[in_, ident], w=[out])

    def act(self, out, in_, func, bias=0.0, scale=1.0, extra_r=()):
        b = bias.ap if isinstance(bias, V) else bias
        sc = scale.ap if isinstance(scale, V) else scale
        r = [in_] + [x for x in (bias, scale) if isinstance(x, V)] + list(extra_r)
        return self.add("act", lambda e: e.activation(out.ap, in_.ap, func, bias=b, scale=sc), r=r, w=[out])

    def ts(self, eng, out, in0, s1, s2, op0, op1=None):
        a1 = s1.ap if isinstance(s1, V) else s1
        a2 = s2.ap if isinstance(s2, V) else s2
        r = [in0] + [x for x in (s1, s2) if isinstance(x, V)]
        if op1 is None and eng == "pool" and op0 == ALU.mult:
            return self.add(eng, lambda e: e.tensor_scalar(out.ap, in0.ap, a1, 0.0, ALU.mult, ALU.add), r=r, w=[out])
        if op1 is None:
            return self.add(eng, lambda e: e.tensor_scalar(out.ap, in0.ap, a1, None, op0), r=r, w=[out])
        return self.add(eng, lambda e: e.tensor_scalar(out.ap, in0.ap, a1, a2, op0, op1), r=r, w=[out])

    def tt(self, eng, out, in0, in1, op):
        return self.add(eng, lambda e: e.tensor_tensor(out.ap, in0.ap, in1.ap, op), r=[in0, in1], w=[out])

    def stt(self, out, in0, sc, in1, op0, op1):
        a = sc.ap if isinstance(sc, V) else sc
        r = [in0, in1] + ([sc] if isinstance(sc, V) else [])
        return self.add("dve", lambda e: e.scalar_tensor_tensor(out.ap, in0.ap, a, in1.ap, op0, op1), r=r, w=[out])

    def copy(self, eng, out, in_):
        if eng == "act":
            return self.add("act", lambda e: e.copy(out.ap, in_.ap), r=[in_], w=[out])
        return self.add(eng, lambda e: e.tensor_copy(out.ap, in_.ap), r=[in_], w=[out])

    def recip(self, out, in_):
        return self.add("dve", lambda e: e.reciprocal(out.ap, in_.ap), r=[in_], w=[out])

    def memset(self, eng, out, val):
        return self.add(eng, lambda e: e.memset(out.ap, val), w=[out])

    def dma(self, out_ap, in_ap, r=(), w=(), dbuf=None, group=None, eng="sp", **kw):
        return self.add(eng, lambda e: e.dma_start(out=out_ap, in_=in_ap, **kw), r=r, w=w, dbuf=dbuf, group=group)

    def emit(self, nc):
        ops = self.ops
        for i, op in enumerate(ops):
            for d in op.deps:
                dop = ops[d]
                if dop.is_dma:
                    continue
                if dop.eng == "pe" and op.eng == "pe":
                    continue
                dop.signal = True
        cnt = {e: 0 for e in self.ENGS}
        for op in ops:
            if op.signal and not op.is_dma:
                if op.fn is None:
                    raise RuntimeError("barrier op cannot signal")
                cnt[op.eng] += 1
                op.cnt = cnt[op.eng]
        with ExitStack() as es:
            sems = {}
            for e in self.ENGS:
                n = (cnt[e] + SEG - 1) // SEG
                sems[e] = [es.enter_context(nc.semaphore(f"s_{e}{k}")) for k in range(n)]
            dsem = {}
            gsem = {}
            for op in ops:
                if op.dbuf is not None and id(op.dbuf) not in dsem:
                    dsem[id(op.dbuf)] = es.enter_context(nc.semaphore(f"d{len(dsem)}"))
                if op.group is not None and op.group not in gsem:
                    gsem[op.group] = es.enter_context(nc.semaphore(f"g{len(gsem)}"))
            self.n_sems = sum(len(v) for v in sems.values()) + len(dsem) + len(gsem)
            block = es.enter_context(nc.Block())

            def run(ename, eng):
                waited = {}
                for i, op in enumerate(ops):
                    if op.eng != ename:
                        continue
                    for d in sorted(op.deps):
                        dop = ops[d]
                        if dop.is_dma:
                            if dop.dbuf is not None:
                                key = ("d", id(dop.dbuf))
                                sem = dsem[id(dop.dbuf)]
                                val = dop.dval
                            else:
                                key = ("g", dop.group)
                                sem = gsem[dop.group]
                                val = 16 * len(self.groups[dop.group])
                        else:
                            if dop.eng == "pe" and ename == "pe":
                                continue
                            c = dop.cnt - 1
                            key = (dop.eng, c // SEG)
                            sem = sems[dop.eng][c // SEG]
                            val = (c % SEG) + 1
                        if waited.get(key, 0) >= val:
                            continue
                        waited[key] = val
                        eng.wait_ge(sem, val)
                    if op.fn is None:
                        continue
                    ins = op.fn(eng)
                    if op.is_dma:
                        if op.dbuf is not None:
                            ins.then_inc(dsem[id(op.dbuf)], 16)
                        else:
                            ins.then_inc(gsem[op.group], 16)
                    elif op.signal:
                        c = op.cnt - 1
                        ins.then_inc(sems[ename][c // SEG], 1)

            @block.tensor
            def _(eng):
                run("pe", eng)

            @block.scalar
            def _(eng):
                run("act", eng)

            @block.vector
            def _(eng):
                run("dve", eng)

            @block.gpsimd
            def _(eng):
                run("pool", eng)

            @block.sync
            def _(eng):
                run("sp", eng)


class Builder:
    def __init__(self, nc, debug_layers=2):
        self.nc = nc
        self.P = Prog()
        self.es = ExitStack()
        self.n = 0
        self.debug_layers = debug_layers

    def sb(self, shape, dt, name=None, stack=None):
        self.n += 1
        nm = f"{name or 't'}_{self.n}"
        h = (stack or self.es).enter_context(self.nc.sbuf_tensor(nm, list(shape), dt))
        return T(h[:] if len(shape) == 2 else h[tuple(slice(None) for _ in shape)], nm)

    def ps(self, shape, dt, name=None):
        self.n += 1
        nm = f"{name or 'p'}_{self.n}"
        h = self.es.enter_context(self.nc.psum_tensor(nm, list(shape), dt))
        return T(h[tuple(slice(None) for _ in shape)], nm)

    def dram_in(self, name, shape, dt):
        return self.nc.dram_tensor(name, list(shape), dt, kind="ExternalInput").ap()

    def setup_common(self):
        nc, P = self.nc, self.P
        self.x_d = self.dram_in("x", [S, D], F32)
        self.pos_d = self.dram_in("pos", [1, S], I32)
        self.out_d = nc.dram_tensor("out", [S, D], F32, kind="ExternalOutput").ap()
        self.c_ident = self.dram_in("c_ident", [128, 128], F32)
        self.c_invf = self.dram_in("c_invf", [128, 2], F32)
        self.c_sgn = self.dram_in("c_sgn", [128, 2], F32)
        self.c_perm = self.dram_in("c_perm", [128, 256], F32)

        self.xT = self.sb([128, 8, S], BF16, "xT")
        self.xT_tb = [T(self.xT.ap[:, :, tb * 512:(tb + 1) * 512], f"xT{tb}") for tb in range(4)]
        self.acc = self.sb([128, NT, D], F32, "acc")
        self.acc_t = [T(self.acc.ap[:, i, :], f"acc{i}") for i in range(NT)]
        self.ident_f = self.sb([128, 128], F32, "identf")
        self.ident_b = self.sb([128, 128], BF16, "identb")
        self.ones_b = self.sb([128, 128], BF16, "ones")
        self.invf = self.sb([128, 2], F32, "invf")
        self.sgn = self.sb([128, 2], F32, "sgn")
        self.perm_f = self.sb([128, 256], F32, "permf")
        self.perm_b = self.sb([128, 256], BF16, "permb")
        self.wst = [self.sb([128, 1152], F32, f"wst{i}") for i in range(2)]
        self.wbf = [self.sb([128, 1152], BF16, f"wbf{i}") for i in range(2)]
        self.wst_i = 0
        self.wbf_i = 0
        self.ztmp = [self.sb([128, 512], F32, f"ztmp{i}") for i in range(2)]
        self.pb = [self.ps([128, 512], F32, f"pb{i}") for i in range(8)]

        P.dma(self.ident_f.ap, self.c_ident, w=[self.ident_f], group="const")
        P.dma(self.invf.ap, self.c_invf, w=[self.invf], group="const")
        P.dma(self.sgn.ap, self.c_sgn, w=[self.sgn], group="const")
        P.dma(self.perm_f.ap, self.c_perm, w=[self.perm_f], group="const")
        P.copy("dve", self.ident_b, self.ident_f)
        P.copy("dve", self.perm_b, self.perm_f)
        P.memset("dve", self.ones_b, 1.0)

    def rope_tables(self, col, st):
        P = self.P
        self.cos_t = self.sb([128, S], F32, "cos", st)
        self.sin_t = self.sb([128, S], F32, "sin", st)
        st2 = ExitStack()
        t = self.sb([128, 512], F32, "rt_t", st2)
        nf = self.sb([128, 512], F32, "rt_f", st2)
        pi_ = self.sb([128, 512], I32, "rt_pi", st2)
        pf_ = self.sb([128, 512], F32, "rt_pf", st2)
        for tb in range(4):
            sl = slice(tb * 512, (tb + 1) * 512)
            P.dma(pi_.ap, self.pos_d.partition_broadcast(128)[:, 0, sl], w=[pi_], dbuf=pi_)
            P.copy("dve", pf_, pi_)
            for which, off in ((self.sin_t, 0.0), (self.cos_t, 0.25)):
                P.ts("dve", t, pf_, self.invf[:, col:col + 1], 1.0 / (2 * PI), ALU.mult, ALU.mult)
                if off:
                    P.ts("dve", t, t, off, None, ALU.add)
                P.ts("dve", nf, t, 8388608.0, None, ALU.add)
                P.ts("dve", nf, nf, -8388608.0, None, ALU.add)
                P.tt("dve", t, t, nf, ALU.subtract)
                k = 1.0 - 1e-6
                P.act(which[:, sl], t, AF.Sin, scale=2 * PI * k)
        P.ts("dve", self.sin_t, self.sin_t, self.sgn[:, col:col + 1], None, ALU.mult)
        P.barrier()
        st2.close()

    def load_w(self, src_ap, shape, eng_cast="pool"):
        P = self.P
        n = int(np.prod(shape[1:]))
        assert n <= 1152, shape
        st = self.wst[self.wst_i % 2]
        self.wst_i += 1
        wb = self.wbf[self.wbf_i % 2]
        self.wbf_i += 1
        if len(shape) == 3:
            sv = st.ap[:, 0:n].rearrange("p (a b) -> p a b", a=shape[1])
            wv = wb.ap[:, 0:n].rearrange("p (a b) -> p a b", a=shape[1])
        else:
            sv = st.ap[:, 0:n]
            wv = wb.ap[:, 0:n]
        if shape[0] < 128:
            sv = sv[0:shape[0]]
            wv = wv[0:shape[0]]
        P.dma(sv, src_ap, w=[st], dbuf=st)
        P.add(eng_cast, lambda e: e.tensor_copy(wv, sv), r=[st], w=[wb])
        return V(wb, wv)

    def load_xT(self, first):
        P = self.P
        st = ExitStack()
        xin = [self.sb([128, D], F32, f"xin{i}", st) for i in range(2)]
        xbf = [self.sb([128, D], BF16, f"xbf{i}", st) for i in range(2)]
        tp = [self.pb[6], self.pb[7]]
        for i in range(NT):
            if first:
                src = xin[i % 2]
                P.dma(src.ap, self.x_d[i * 128:(i + 1) * 128, :], w=[src], dbuf=src)
                P.ts("pool", self.acc_t[i], src, ALPHA, None, ALU.mult)
            else:
                src = self.acc_t[i]
            xb = xbf[i % 2]
            P.copy("act", xb, src)
            for half in range(2):
                pbk = tp[half]
                pv = V(pbk, pbk.ap.bitcast(BF16))
                for c in range(4):
                    cc = half * 4 + c
                    P.transpose(pv[:, c * 128:(c + 1) * 128], xb[:, cc * 128:(cc + 1) * 128], self.ident_b)
                dst = self.xT_tb[i // 4][:, half * 4:(half + 1) * 4, (i % 4) * 128:(i % 4 + 1) * 128]
                srcv = V(pbk, pbk.ap.bitcast(BF16)[:, 0:512].rearrange("p (c t) -> p c t", c=4))
                P.copy("dve", dst, srcv)
            if not first:
                P.ts("pool", self.acc_t[i], self.acc_t[i], ALPHA, None, ALU.mult)
        P.barrier()
        st.close()

    def proj_fm(self, ps_out, w, ncols_sl, tb, nk=8, src=None):
        P = self.P
        for k in range(nk):
            rhs = (self.xT_tb[tb][:, k, :] if src is None else src[:, k, tb * 512:(tb + 1) * 512])
            P.matmul(ps_out, w[:, k, ncols_sl], rhs, start=(k == 0), stop=(k == nk - 1))

    def rope_apply(self, dst, ps_x, tb, d, st_tmp, perm=None):
        P = self.P
        xb, ps_r, t1, t2 = st_tmp
        if perm is None:
            perm = self.perm_b[:, 0:128] if d == 128 else self.perm_b[0:64, 128:192]
        tsl = slice(tb * 512, (tb + 1) * 512)
        P.copy("dve", xb[0:d, :], ps_x)
        P.matmul(ps_r[0:d, :], perm, xb[0:d, :])
        P.tt("dve", t1[0:d, :], ps_x, self.cos_t[0:d, tsl], ALU.mult)
        P.tt("dve", t2[0:d, :], ps_r[0:d, :], self.sin_t[0:d, tsl], ALU.mult)
        P.tt("pool", dst, t1[0:d, :], t2[0:d, :], ALU.add)

    def rope_proj4(self, dst, w, wsl, nk, src, d, perm=None):
        P = self.P
        if perm is None:
            perm = self.perm_b[:, 0:128] if d == 128 else self.perm_b[0:64, 128:192]
        sets = self.rope_sets

        def A(tb):
            s_ = sets[tb % 2]
            self.proj_fm(s_["ps_x"][0:d, :], w, wsl, tb, nk=nk, src=src)
            P.copy("dve", s_["xb"][0:d, :], s_["ps_x"][0:d, :])

        def B(tb):
            s_ = sets[tb % 2]
            tsl = slice(tb * 512, (tb + 1) * 512)
            P.matmul(s_["ps_r"][0:d, :], perm, s_["xb"][0:d, :])
            P.tt("dve", s_["t1"][0:d, :], s_["ps_x"][0:d, :], self.cos_t[0:d, tsl], ALU.mult)
            P.tt("dve", s_["t2"][0:d, :], s_["ps_r"][0:d, :], self.sin_t[0:d, tsl], ALU.mult)
            P.tt("pool", dst[0:d, tsl], s_["t1"][0:d, :], s_["t2"][0:d, :], ALU.add)
        A(0); A(1); B(0); A(2); B(1); A(3); B(2); B(3)

    def make_rope_sets(self, st):
        self.rope_sets = [
            dict(xb=self.sb([128, 512], BF16, "r_xb0", st), t1=self.sb([128, 512], F32, "r_t1", st),
                 t2=self.sb([128, 512], F32, "r_t2", st), ps_x=self.pb[0], ps_r=self.pb[2]),
            dict(xb=self.sb([128, 512], BF16, "r_xb1", st), t1=self.ztmp[0], t2=self.ztmp[1],
                 ps_x=self.pb[1], ps_r=self.pb[7])]

    def out_proj_pair(self, yT, w_out_d, pair):
        P = self.P
        for half in range(2):
            w = self.load_w(w_out_d[pair * 256:(pair + 1) * 256, half * 512:(half + 1) * 512]
                            .rearrange("(a p) c -> p a c", p=128), [128, 2, 512])
            for i in range(NT):
                pbk = self.pb[i % 2]
                for j in range(2):
                    P.matmul(pbk, yT[:, j, i * 128:(i + 1) * 128], w[:, j, :], start=(j == 0), stop=(j == 1))
                a = self.acc_t[i][:, half * 512:(half + 1) * 512]
                P.tt("dve", a, a, pbk, ALU.add)

    def layer_norm_out(self, lnw_d, lnb_d, last):
        P = self.P
        st = ExitStack()
        lw = self.sb([128, D], F32, "lnw", st)
        lb = self.sb([128, D], F32, "lnb", st)
        P.dma(lw.ap, lnw_d.partition_broadcast(128)[:, 0, :], w=[lw], dbuf=lw)
        P.dma(lb.ap, lnb_d.partition_broadcast(128)[:, 0, :], w=[lb], dbuf=lb)
        junk = [self.sb([128, D], F32, f"junk{i}", st) for i in range(2)]
        stat = [self.sb([128, 8], F32, f"stat{i}", st) for i in range(2)]
        odone = []
        for i in range(NT):
            a = self.acc_t[i]
            sm = stat[i % 2]
            jk = junk[i % 2]
            P.add("dve", lambda e, sm=sm, a=a: e.reduce_sum(sm.ap[:, 0:1], a.ap, mybir.AxisListType.X), r=[a], w=[sm])
            P.ts("dve", sm[:, 1:2], sm[:, 0:1], -1.0 / D, None, ALU.mult)
            P.ts("dve", a, a, sm[:, 1:2], None, ALU.add)
            P.act(jk, a, AF.Square)
            P.add("dve", lambda e, sm=sm, jk=jk: e.reduce_sum(sm.ap[:, 2:3], jk.ap, mybir.AxisListType.X), r=[jk], w=[sm])
            P.act(sm[:, 3:4], sm[:, 2:3], AF.Sqrt, bias=self.eps_ln, scale=1.0 / D)
            P.recip(sm[:, 4:5], sm[:, 3:4])
            P.stt(a, a, sm[:, 4:5], lw, ALU.mult, ALU.mult)
            P.tt("pool", a, a, lb, ALU.add)
            if last:
                odone.append(P.dma(self.out_d[i * 128:(i + 1) * 128, :], a.ap, r=[a], dbuf=a))
        if last:
            fin = P.add("sp", None)
            P.ops[fin].deps.update(odone)
        P.barrier()
        st.close()

    def attention_head(self, qT, kT, vtok, yT_dst, zs, scale, q2=None, k2=None, mask=None):
        P = self.P
        pts = self.pt_tiles
        LAG = len(pts) - 1
        sb_ps = [self.pb[0], self.pb[1], self.pb[2]] + ([self.pb[7]] if LAG >= 3 else [])
        stms = self.ztmp + ([self.stm_extra] if LAG >= 3 else [])
        steps = []
        for qb in range(4):
            kts = []
            for kt in range(NT):
                if mask is not None:
                    q0, k0 = qb * 512, kt * 128
                    if k0 + 127 < q0 - 1024 or k0 > q0 + 511 + 1024:
                        continue
                kts.append(kt)
            for j, kt in enumerate(kts):
                steps.append((qb, kt, j == 0, j == len(kts) - 1))
        n = len(steps)
        for i in range(n + LAG):
            if i < n:
                qb, kt, first, last = steps[i]
                sps = sb_ps[i % len(sb_ps)]
                qsl = slice(qb * 512, (qb + 1) * 512)
                ksl = slice(kt * 128, (kt + 1) * 128)
                P.matmul(sps, kT[:, ksl], qT[:, qsl], start=True, stop=(q2 is None))
                if q2 is not None:
                    P.matmul(sps, k2[:, ksl], q2[:, qsl], start=False, stop=True)
                pt = pts[i % len(pts)]
                stm = stms[i % len(stms)]
                if mask is not None:
                    o = qb * 512 - kt * 128 + 2048
                    P.stt(stm, sps, scale, mask[:, o:o + 512], ALU.mult, ALU.add)
                    P.act(pt, stm, AF.Exp)
                else:
                    P.copy("dve", stm, sps)
                    P.act(pt, stm, AF.Exp, scale=scale)
            if i >= LAG:
                qb, kt, first, last = steps[i - LAG]
                pt = pts[(i - LAG) % len(pts)]
                ops_ = self.pb[3 + (qb % 2)]
                dps = self.pb[5 + (qb % 2)]
                P.matmul(ops_, vtok[:, kt, :], pt, start=first, stop=last)
                P.matmul(dps, self.ones_b, pt, start=first, stop=last)
                if last:
                    qsl = slice(qb * 512, (qb + 1) * 512)
                    rd = self.att_rd[0]
                    P.copy("dve", rd, dps)
                    P.act(rd, rd, AF.Ln)
                    P.act(rd, rd, AF.Exp, scale=-1.0)
                    P.tt("dve", rd, ops_, rd, ALU.mult)
                    P.tt("pool", yT_dst[:, qsl], rd, zs[:, qsl], ALU.mult)

    def silu_z(self, w_in_d, col, zs):
        P = self.P
        w = self.load_w(w_in_d[:, col:col + 128].rearrange("(k p) c -> p k c", p=128), [128, 8, 128])
        for tb in range(4):
            pbk = self.pb[7] if tb % 2 else self.pb[6]
            self.proj_fm(pbk, w, slice(0, 128), tb)
            zt = self.ztmp[tb % 2]
            P.copy("dve", zt, pbk)
            P.act(zs[:, tb * 512:(tb + 1) * 512], zt, AF.Silu)

    def layer_even(self):
        nc, P = self.nc, self.P
        w_in = self.dram_in("even_w_in", [D, 6208], F32)
        q_norm = self.dram_in("even_q_norm", [1, 768], F32)
        w_uq = self.dram_in("even_w_uq", [768, 1536], F32)
        kv_norm = self.dram_in("even_kv_norm", [1, 256], F32)
        w_ukv = self.dram_in("even_w_ukv", [256, 2048], F32)
        w_out = self.dram_in("even_w_out", [2048, D], F32)
        ln_w = self.dram_in("even_ln_w", [1, D], F32)
        ln_b = self.dram_in("even_ln_b", [1, D], F32)
        c_mask = self.dram_in("c_dilmask", [128, 4096], F32)

        self.load_xT(first=True)
        import os
        stop = int(os.environ.get("KSTOP", "99"))
        if stop == 1:
            self.layer_norm_out(ln_w[0:1, :], ln_b[0:1, :], last=True)
            return

        st = ExitStack()
        def bail():
            P.barrier()
            st.close()
            self.layer_norm_out(ln_w[0:1, :], ln_b[0:1, :], last=True)
        if stop != 22:
            self.rope_tables(0, st)
        if stop == 21:
            return bail()
        qn_w = self.sb([128, 6], F32, "qnw", st)
        kvn_w = self.sb([128, 2], F32, "kvnw", st)
        P.dma(qn_w.ap, q_norm[0, :].rearrange("(c p) -> p c", p=128), w=[qn_w], dbuf=qn_w, allow_slow_non_contiguous=True)
        P.dma(kvn_w.ap, kv_norm[0, :].rearrange("(c p) -> p c", p=128), w=[kvn_w], dbuf=kvn_w, allow_slow_non_contiguous=True)
        if stop == 22:
            return bail()
        cq = self.sb([128, 6, S], BF16, "cq", st)
        ckv = self.sb([128, 2, S], BF16, "ckv", st)
        kr = self.sb([128, S], BF16, "krope", st)
        P.memset("pool", kr[64:128, :], 0.0)
        self.make_rope_sets(st)
        self.pt_tiles = [self.sb([128, 512], BF16, f"pt{i}", st) for i in range(3)]
        self.att_rd = [self.sb([128, 512], F32, f"attrd{i}", st) for i in range(1)]

        st3 = ExitStack()
        sq = [self.sb([128, 512], BF16, f"sq{i}", st3) for i in range(2)]
        rstd = self.sb([128, 512], F32, "rstd", st3)
        def lat(dst, ncks, col0, nw, fdim):
            cnt = 0
            for ck in range(ncks):
                w = self.load_w(w_in[:, col0 + ck * 128: col0 + (ck + 1) * 128].rearrange("(k p) c -> p k c", p=128),
                                [128, 8, 128])
                if stop == 231:
                    continue
                for tb in range(4):
                    pbk = self.pb[cnt % 2]
                    self.proj_fm(pbk, w, slice(0, 128), tb)
                    s_ = sq[cnt % 2]
                    P.copy("dve", dst[:, ck, tb * 512:(tb + 1) * 512], pbk)
                    P.act(s_, dst[:, ck, tb * 512:(tb + 1) * 512], AF.Square)
                    if stop != 232:
                        P.matmul(self.pb[3 + tb], self.ones_b, s_, start=(ck == 0), stop=(ck == ncks - 1))
                    cnt += 1
            if stop in (231, 232, 233):
                return
            for tb in range(4):
                P.copy("dve", rstd, self.pb[3 + tb])
                P.act(rstd, rstd, AF.Ln, bias=self.eps_rms, scale=1.0 / fdim)
                P.act(rstd, rstd, AF.Exp, scale=-0.5)
                for ck in range(ncks):
                    dv = dst[:, ck, tb * 512:(tb + 1) * 512]
                    P.stt(dv, dv, nw[:, ck:ck + 1], rstd, ALU.mult, ALU.mult)

        lat(cq, 6, 0, qn_w, 768.0)
        lat(ckv, 2, 768, kvn_w, 256.0)
        P.barrier()
        st3.close()
        if stop in (23, 231, 232, 233):
            return bail()
        w = self.load_w(w_in[:, 1024:1088].rearrange("(k p) c -> p k c", p=128), [128, 8, 64])
        self.rope_proj4(kr, w, slice(0, 64), 8, None, 64)

        if stop == 2:
            P.barrier()
            st.close()
            self.layer_norm_out(ln_w[0:1, :], ln_b[0:1, :], last=True)
            return
        qn = self.sb([128, S], BF16, "qn", st)
        qr = self.sb([128, S], BF16, "qr", st)
        P.memset("pool", qr[64:128, :], 0.0)
        kn = self.sb([128, S], BF16, "kn", st)
        vt = self.sb([128, NT, 128], BF16, "vt", st)
        zs = self.sb([128, S], BF16, "zs", st)
        yT = self.sb([128, 2, S], BF16, "yT", st)
        zoff = 768 + 256 + 64 + 3 * 1024
        scale_mla = 192.0 ** -0.5
        for h in range(8):
            wq = self.load_w(w_uq[:, h * 192:(h + 1) * 192].rearrange("(k p) c -> p k c", p=128), [128, 6, 192])
            for tb in range(4):
                pbk = self.pb[3 + tb % 2]
                self.proj_fm(pbk, wq, slice(0, 128), tb, nk=6, src=cq)
                P.copy("dve", qn[:, tb * 512:(tb + 1) * 512], pbk)
            self.rope_proj4(qr, wq, slice(128, 192), 6, cq, 64)
            wkv = self.load_w(w_ukv[:, h * 256:(h + 1) * 256].rearrange("(k p) c -> p k c", p=128), [128, 2, 256])
            for tb in range(4):
                pbk = self.pb[tb % 2]
                self.proj_fm(pbk, wkv, slice(0, 128), tb, nk=2, src=ckv)
                P.copy("dve", kn[:, tb * 512:(tb + 1) * 512], pbk)
            for g in range(4):
                pbk = self.pb[g % 2]
                for j in range(4):
                    i = g * 4 + j
                    for k in range(2):
                        P.matmul(pbk[:, j * 128:(j + 1) * 128], ckv[:, k, i * 128:(i + 1) * 128], wkv[:, k, 128:256],
                                 start=(k == 0), stop=(k == 1))
                P.copy("dve", V(vt, vt.ap[:, g * 4:(g + 1) * 4, :]),
                       V(pbk, pbk.ap.rearrange("p (a b) -> p a b", a=4)))
            self.silu_z(w_in, zoff + h * 128, zs)
            self.attention_head(qn, kn, vt, yT[:, h % 2, :], zs, scale_mla, q2=qr, k2=kr)
            if h % 2 == 1:
                self.out_proj_pair(yT, w_out, h // 2)
            if stop == 3 and h == 1:
                break
        P.barrier()
        st.close()
        if stop == 3:
            self.layer_norm_out(ln_w[0:1, :], ln_b[0:1, :], last=True)
            return

        st = ExitStack()
        self.rope_tables(1, st)
        mask = self.sb([128, 4096], F32, "mask", st)
        P.dma(mask.ap, c_mask, w=[mask], dbuf=mask)
        self.make_rope_sets(st)
        self.pt_tiles = [self.sb([128, 512], BF16, f"pt{i}", st) for i in range(4)]
        self.stm_extra = self.sb([128, 512], F32, "stm_x", st)
        self.att_rd = [self.sb([128, 512], F32, f"attrd{i}", st) for i in range(1)]
        qn = self.sb([128, S], BF16, "qn", st)
        kn = self.sb([128, S], BF16, "kn", st)
        vt = self.sb([128, NT, 128], BF16, "vt", st)
        zs = self.sb([128, S], BF16, "zs", st)
        yT = self.sb([128, 2, S], BF16, "yT", st)
        qoff, koff, voff = 1088, 1088 + 1024, 1088 + 2048
        scale_d = 128.0 ** -0.5
        for h in range(8):
            for (dst, off) in ((qn, qoff), (kn, koff)):
                w = self.load_w(w_in[:, off + h * 128: off + (h + 1) * 128].rearrange("(k p) c -> p k c", p=128),
                                [128, 8, 128])
                self.rope_proj4(dst, w, slice(0, 128), 8, None, 128)
            w = self.load_w(w_in[:, voff + h * 128: voff + (h + 1) * 128].rearrange("(k p) c -> p k c", p=128),
                            [128, 8, 128])
            for g in range(4):
                pbk = self.pb[g % 2]
                for j in range(4):
                    i = g * 4 + j
                    for k in range(8):
                        P.matmul(pbk[:, j * 128:(j + 1) * 128],
                                 self.xT_tb[i // 4][:, k, (i % 4) * 128:(i % 4 + 1) * 128], w[:, k, :],
                                 start=(k == 0), stop=(k == 7))
                P.copy("dve", V(vt, vt.ap[:, g * 4:(g + 1) * 4, :]),
                       V(pbk, pbk.ap.rearrange("p (a b) -> p a b", a=4)))
            self.silu_z(w_in, zoff + 1024 + h * 128, zs)
            self.attention_head(qn, kn, vt, yT[:, h % 2, :], zs, scale_d, mask=mask)
            if h % 2 == 1:
                self.out_proj_pair(yT, w_out, 4 + h // 2)
        P.barrier()
        st.close()
        self.layer_norm_out(ln_w[0:1, :], ln_b[0:1, :], last=(self.debug_layers == 1))

    def consts_small(self):
        P = self.P
        self.cst = self.sb([128, 8], F32, "cst")
        self.mpi = self.cst[:, 0:1]
        self.eps_rms = self.cst[:, 1:2]
        self.eps_ln = self.cst[:, 2:3]
        P.memset("pool", self.cst[:, 0:1], -PI * (1.0 - 1e-6))
        P.memset("pool", self.cst[:, 1:2], 1e-6)
        P.memset("pool", self.cst[:, 2:3], 1e-5)
        self.one_c = self.cst[:, 3:4]
        P.memset("pool", self.cst[:, 3:4], 1.0)

    def build(self):
        self.setup_common()
        self.consts_small()
        self.layer_even()
        if self.debug_layers >= 2:
            self.layer_odd()
        self.P.emit(self.nc)
        self.es.close()


    def layer_odd(self):
        nc, P = self.nc, self.P
        w_in = self.dram_in("odd_w_in", [D, 7200], F32)
        conv_w = self.dram_in("odd_conv_w", [5, 3072], F32)
        a_log = self.dram_in("odd_a_log", [2, 8], F32)
        dt_bias = self.dram_in("odd_dt_bias", [2, 8], F32)
        gdn_norm = self.dram_in("odd_gdn_norm", [1, 128], F32)
        rn_w = self.dram_in("odd_ret_norm_w", [1, 1024], F32)
        rn_b = self.dram_in("odd_ret_norm_b", [1, 1024], F32)
        w_out = self.dram_in("odd_w_out", [2048, D], F32)
        ln_w = self.dram_in("odd_ln_w", [1, D], F32)
        ln_b = self.dram_in("odd_ln_b", [1, D], F32)
        c_gm = self.dram_in("c_gmask", [128, 1024], F32)
        c_perm2 = self.dram_in("c_perm2", [128, 128], F32)
        c_dec = self.dram_in("c_retdecay", [8, 128, 4096], F32)
        X = mybir.AxisListType.X

        self.load_xT(first=False)
        zoff = 5152

        st = ExitStack()
        gm = self.sb([128, 1024], F32, "gm", st)
        P.dma(gm.ap, c_gm, w=[gm], dbuf=gm)
        trif, trib = gm[:, 0:128], gm[:, 128:256]
        mSL, mSU, mLI, mUI, nSL, nSU = (gm[:, (2 + i) * 128:(3 + i) * 128] for i in range(6))
        ones_f = self.sb([128, 128], F32, "onesf", st)
        P.memset("pool", ones_f, 1.0)
        gnw = self.sb([128, 1], F32, "gnw", st)
        P.dma(gnw.ap, gdn_norm[0, :].rearrange("(p o) -> p o", o=1), w=[gnw], dbuf=gnw)
        cols = {nm: self.sb([128, NT, 16], F32, "c_" + nm, st)
                for nm in ("beta", "nbeta", "gc", "ec", "kd", "wc", "glx")}
        st0 = ExitStack()
        bat = self.sb([128, NT, 32], F32, "bat", st0)
        gtk = self.sb([128, NT, 16], F32, "gtk", st0)
        gtot = self.sb([128, NT, 16], F32, "gtot", st0)
        alb = self.sb([128, 16], F32, "alb", st0)
        dtb = self.sb([128, 16], F32, "dtb", st0)
        P.dma(alb.ap, a_log.rearrange("a b -> (a b)").partition_broadcast(128), w=[alb], dbuf=alb)
        P.dma(dtb.ap, dt_bias.rearrange("a b -> (a b)").partition_broadcast(128), w=[dtb], dbuf=dtb)
        wba = self.load_w(w_in[:, 3072:3104].rearrange("(k p) c -> p k c", p=128), [128, 8, 32])
        pbk = self.pb[0]
        for t in range(NT):
            for k in range(8):
                P.matmul(pbk[:, t * 32:(t + 1) * 32], self.xT_tb[t // 4][:, k, (t % 4) * 128:(t % 4 + 1) * 128],
                         wba[:, k, :], start=(k == 0), stop=(k == 7))
        P.copy("dve", V(bat, bat.ap.rearrange("p t e -> p (t e)")), pbk)
        P.act(cols["beta"], bat[:, :, 0:16], AF.Sigmoid)
        P.ts("dve", cols["nbeta"], cols["beta"], -1.0, None, ALU.mult)
        P.act(alb, alb, AF.Exp)
        P.ts("dve", alb, alb, -1.0, None, ALU.mult)
        for t in range(NT):
            P.tt("dve", gtk[:, t, :], bat[:, t, 16:32], dtb, ALU.add)
        P.act(gtk, gtk, AF.Exp)
        P.act(gtk, gtk, AF.Ln, bias=self.one_c)
        for t in range(NT):
            P.tt("dve", gtk[:, t, :], gtk[:, t, :], alb, ALU.mult)
        pb1, pb2 = self.pb[1], self.pb[2]
        for t in range(NT):
            P.matmul(pb1[:, t * 16:t * 16 + 8], trif, gtk[:, t, 0:8])
            P.matmul(pb1[:, t * 16 + 8:t * 16 + 16], trib, gtk[:, t, 8:16])
            P.matmul(pb2[:, t * 16:(t + 1) * 16], ones_f, gtk[:, t, :])
        P.copy("dve", V(cols["gc"], cols["gc"].ap.rearrange("p t e -> p (t e)")), pb1[:, 0:256])
        P.copy("dve", V(gtot, gtot.ap.rearrange("p t e -> p (t e)")), pb2[:, 0:256])
        P.act(cols["ec"], cols["gc"], AF.Exp)
        P.tt("dve", cols["wc"], cols["ec"], cols["beta"], ALU.mult)
        P.act(cols["glx"], gtot, AF.Exp)
        P.tt("dve", gtot, gtot, cols["gc"], ALU.subtract)
        P.act(cols["kd"], gtot, AF.Exp)
        P.barrier()
        st0.close()

        pre = self.sb([128, S + 4], BF16, "pre", st)
        P.memset("pool", pre, 0.0)
        qT = self.sb([128, S], BF16, "gqT", st)
        kT = self.sb([128, S], BF16, "gkT", st)
        ktok = self.sb([128, NT, 128], BF16, "ktok", st)
        vtok = self.sb([128, NT, 128], BF16, "vtok", st)
        cw = self.sb([128, 5], F32, "cw", st)
        dgm = self.sb([128, 5, 128], BF16, "dgm", st)
        oT = self.sb([128, S], F32, "goT", st)
        oT_b = [T(oT.ap[:, i * 512:(i + 1) * 512], f"oTb{i}") for i in range(4)]
        oT_t = [V(oT_b[i // 4], oT.ap[:, i * 128:(i + 1) * 128]) for i in range(NT)]
        zs = self.sb([128, S], BF16, "zs", st)
        vT = zs
        yT = self.sb([128, 2, S], BF16, "yT", st)
        sqb = self.sb([128, 512], BF16, "sqb", st)
        sqb2 = [sqb, V(pre, pre.ap[:, 1028:1540])]
        rst = self.sb([128, 512], F32, "rst", st)
        rst2 = V(pre, pre.ap[:, 4:1028].bitcast(F32))
        Sf = [self.sb([128, 128], F32, f"Sf{c}", st) for c in range(2)]
        Sb = [self.sb([128, 128], BF16, f"Sb{c}", st) for c in range(2)]
        NL = 4

        def f32t(nm):
            return self.sb([128, 128], F32, nm, st)

        def b16t(nm):
            return self.sb([128, 128], BF16, nm, st)
        lanes = []
        for l in range(NL):
            ln = dict(A=f32t(f"lA{l}"), B=f32t(f"lB{l}"), C=f32t(f"lC{l}"), H=f32t(f"lH{l}"), Pt=f32t(f"lP{l}"),
                      XX=[self.sb([128, 256], F32, f"lX{l}{i}", st) for i in range(2)],
                      Ptb=b16t(f"lPb{l}"), kbg=b16t(f"lkb{l}"), vb=b16t(f"lvb{l}"), bank=self.pb[l],
                      out=[dict(wTs=b16t(f"owT{l}{r}"), us=f32t(f"ous{l}{r}"), qg=b16t(f"oqg{l}{r}"),
                                attnT=b16t(f"oat{l}{r}"), kdt=b16t(f"okd{l}{r}")) for r in range(2)])
            lanes.append(ln)
        cbank = [self.pb[4], self.pb[5]]

        def conv_chunk(col, dst, normalize, post_scale):
            w = self.load_w(w_in[:, col:col + 128].rearrange("(k p) c -> p k c", p=128), [128, 8, 128])
            P.dma(cw.ap, conv_w[:, col:col + 128].rearrange("j c -> c j"), w=[cw], dbuf=cw,
                  allow_slow_non_contiguous=True)
            for j in range(5):
                P.ts("dve", dgm[:, j, :], self.ident_b, cw[:, j:j + 1], None, ALU.mult)
            for tb in range(4):
                pk = self.pb[6 + tb % 2]
                self.proj_fm(pk, w, slice(0, 128), tb)
                P.copy("dve", pre[:, 2 + tb * 512: 2 + (tb + 1) * 512], pk)
            for tb in range(4):
                pk = self.pb[6 + tb % 2]
                for j in range(5):
                    P.matmul(pk, dgm[:, j, :], pre[:, tb * 512 + j: tb * 512 + j + 512], start=(j == 0), stop=(j == 4))
                zt = self.ztmp[tb % 2]
                P.copy("dve", zt, pk)
                P.act(dst[:, tb * 512:(tb + 1) * 512], zt, AF.Silu)
            if normalize:
                rl = [rst, rst2, self.ztmp[0], self.ztmp[1]]
                sl4 = [slice(tb * 512, (tb + 1) * 512) for tb in range(4)]
                for tb in range(4):
                    pk = self.pb[6 + tb % 2]
                    P.act(sqb2[tb % 2], dst[:, sl4[tb]], AF.Square)
                    P.matmul(pk, self.ones_b, sqb2[tb % 2])
                    P.copy("dve", rl[tb], pk)
                for tb in range(4):
                    P.act(rl[tb], rl[tb], AF.Ln, bias=self.eps_rms, scale=1.0)
                for tb in range(4):
                    P.act(rl[tb], rl[tb], AF.Exp, scale=-0.5)
                for tb in range(4):
                    P.stt(dst[:, sl4[tb]], dst[:, sl4[tb]], post_scale, rl[tb], ALU.mult, ALU.mult)

        def prep_gen(l, dr, t, r, h):
            ln = lanes[l]
            o = ln["out"][r]
            bk = ln["bank"]
            e = dr * 8 + h
            mN, mA = (mSL, mUI) if dr == 0 else (mSU, mLI)
            tsl = slice(t * 128, (t + 1) * 128)
            c_ = lambda nm: cols[nm][:, t, e:e + 1]
            A, B, C, H, Pt, XX = ln["A"], ln["B"], ln["C"], ln["H"], ln["Pt"], ln["XX"]
            P.ts("pool", A, self.ident_f, c_("gc"), None, ALU.mult)
            P.ts("pool", ln["kbg"], ktok[:, t, :], c_("wc"), None, ALU.mult)
            P.ts("pool", ln["vb"], vtok[:, t, :], c_("beta"), None, ALU.mult)
            P.ts("pool", o["kdt"], ktok[:, t, :], c_("kd"), None, ALU.mult)
            P.matmul(bk[:, 0:128], ones_f, A)
            P.matmul(bk[:, 256:384], kT[:, tsl], kT[:, tsl])
            P.matmul(bk[:, 384:512], kT[:, tsl], qT[:, tsl])
            yield
            P.ts("dve", A, bk[:, 0:128], c_("gc"), 0.0, ALU.subtract, ALU.max)
            P.ts("dve", B, bk[:, 0:128], c_("gc"), 0.0, ALU.subtract, ALU.min)
            P.copy("dve", H, bk[:, 0:128])
            P.act(A, A, AF.Exp, scale=-1.0)
            P.act(B, B, AF.Exp)
            P.act(H, H, AF.Exp)
            yield
            P.tt("dve", C, bk[:, 256:384], A, ALU.mult)
            P.stt(XX[0][:, 0:128], C, c_("nbeta"), mN, ALU.mult, ALU.mult)
            P.matmul(bk[:, 128:256], XX[0][:, 0:128], self.ident_f)
            P.tt("dve", C, bk[:, 384:512], B, ALU.mult)
            P.tt("pool", o["attnT"], C, mA, ALU.mult)
            P.tt("pool", o["qg"], qT[:, tsl], H, ALU.mult)
            yield
            P.copy("dve", XX[0][:, 128:256], bk[:, 128:256])
            P.tt("pool", Pt, XX[0][:, 128:256], self.ident_f, ALU.add)
            cur = XX[0]
            XXb = V(o["us"].t, o["us"].ap.bitcast(BF16))
            X6b = o["wTs"]
            Pb = ln["Ptb"]
            for k in range(1, 7):
                nxt = XX[k % 2]
                if k < 6:
                    P.matmul(bk[:, 0:128], cur[:, 128:256], cur[:, 0:128])
                    P.matmul(bk[:, 128:256], cur[:, 0:128], cur[:, 128:256])
                else:
                    P.matmul(bk[:, 0:128], XXb[:, 128:256], XXb[:, 0:128])
                yield
                if k < 5:
                    P.copy("dve", nxt, bk[:, 0:256])
                    P.matmul(bk[:, 256:384], nxt[:, 0:128], Pt)
                elif k == 5:
                    P.copy("dve", XXb, bk[:, 0:256])
                    P.copy("pool", Pb, Pt)
                    P.matmul(bk[:, 256:384], XXb[:, 0:128], Pb)
                else:
                    P.copy("dve", X6b, bk[:, 0:128])
                    P.copy("pool", Pb, Pt)
                    P.matmul(bk[:, 256:384], X6b, Pb)
                cur = nxt
                yield
                P.tt("dve", Pt, Pt, bk[:, 256:384], ALU.add)
            P.copy("pool", ln["Ptb"], Pt)
            P.matmul(bk[:, 384:512], ln["kbg"], ln["Ptb"])
            P.matmul(bk[:, 256:384], ln["Ptb"], ln["vb"])
            yield
            P.copy("dve", o["wTs"], bk[:, 384:512])
            P.copy("dve", o["us"], bk[:, 256:384])
            yield

        def rec_gen(items, h):
            by_chain = {0: [it for it in items if it[0] == 0], 1: [it for it in items if it[0] == 1]}
            nstep = max(len(v) for v in by_chain.values())
            for sidx in range(nstep):
                cur = [by_chain[c][sidx] for c in (0, 1) if sidx < len(by_chain[c])]
                for (c, l, r, t) in cur:
                    o = lanes[l]["out"][r]
                    P.matmul(cbank[c][:, 0:128], o["wTs"], Sb[c])
                yield
                vn = {}
                for (c, l, r, t) in cur:
                    o = lanes[l]["out"][r]
                    vnb = lanes[l]["C"]
                for (c, l, r, t) in cur:
                    o = lanes[l]["out"][r]
                    P.tt("dve", vnbs[c], o["us"], cbank[c][:, 0:128], ALU.subtract)
                    P.matmul(cbank[c][:, 128:256], Sb[c], o["qg"], start=True, stop=False)
                    P.matmul(cbank[c][:, 128:256], vnbs[c], o["attnT"], start=False, stop=True)
                    P.matmul(cbank[c][:, 256:384], o["kdt"], vnbs[c])
                yield
                for (c, l, r, t) in cur:
                    e = c * 8 + h
                    first = (c == 0 and t <= 7) or (c == 1 and t >= 8)
                    if first:
                        P.copy("dve", oT_t[t], cbank[c][:, 128:256])
                    else:
                        P.tt("dve", oT_t[t], oT_t[t], cbank[c][:, 128:256], ALU.add)
                    P.stt(Sf[c], Sf[c], cols["glx"][:, t, e:e + 1], cbank[c][:, 256:384], ALU.mult, ALU.add)
                    P.copy("pool", Sb[c], Sf[c])
                yield

        vnbs = [b16t(f"vnb{c}") for c in range(2)]

        def run_interleaved(gens):
            gens = [g for g in gens if g is not None]
            while gens:
                alive = []
                for g in gens:
                    try:
                        next(g)
                        alive.append(g)
                    except StopIteration:
                        pass
                gens = alive

        for h in range(8):
            conv_chunk(h * 128, qT, True, 128.0 ** -0.5)
            conv_chunk(1024 + h * 128, kT, True, 1.0)
            conv_chunk(2048 + h * 128, vT, False, 1.0)
            for (srcT, dtok) in ((kT, ktok), (vT, vtok)):
                for g in range(4):
                    pk = self.pb[6 + g % 2]
                    for j in range(4):
                        i = g * 4 + j
                        P.matmul(pk[:, j * 128:(j + 1) * 128], srcT[:, i * 128:(i + 1) * 128], self.ident_b)
                    P.copy("dve", V(dtok, dtok.ap[:, g * 4:(g + 1) * 4, :]), V(pk, pk.ap.rearrange("p (a b) -> p a b", a=4)))
            self.silu_z(w_in, zoff + h * 128, zs)
            for c in range(2):
                P.memset("pool", Sf[c], 0.0)
                P.memset("pool", Sb[c], 0.0)
            ngroups = NT // 2
            prev_rec = None
            for g in range(ngroups + 1):
                gens = []
                if g < ngroups:
                    r = g % 2
                    work = [(0, 0, 2 * g), (1, 1, 15 - 2 * g), (2, 0, 2 * g + 1), (3, 1, 14 - 2 * g)]
                    gens += [prep_gen(l, dr, t, r, h) for (l, dr, t) in work]
                if prev_rec is not None:
                    gens.append(rec_gen(prev_rec, h))
                run_interleaved(gens)
                if g < ngroups:
                    prev_rec = [(dr, l, g % 2, t) for (l, dr, t) in work]
            rl = [rst, rst2, self.ztmp[0], self.ztmp[1]]
            sl4 = [slice(tb * 512, (tb + 1) * 512) for tb in range(4)]
            for tb in range(4):
                pk = self.pb[6 + tb % 2]
                P.act(sqb2[tb % 2], oT_b[tb], AF.Square)
                P.matmul(pk, self.ones_b, sqb2[tb % 2])
                P.copy("dve", rl[tb], pk)
            for tb in range(4):
                P.act(rl[tb], rl[tb], AF.Ln, bias=self.eps_rms, scale=1.0 / 128.0)
            for tb in range(4):
                P.act(rl[tb], rl[tb], AF.Exp, scale=-0.5)
            for tb in range(4):
                P.stt(rl[tb], oT_b[tb], gnw[:, 0:1], rl[tb], ALU.mult, ALU.mult)
                P.tt("pool", yT[:, h % 2, sl4[tb]], rl[tb], zs[:, sl4[tb]], ALU.mult)
            if h % 2 == 1:
                self.out_proj_pair(yT, w_out, h // 2)
        P.barrier()
        st.close()

        st = ExitStack()
        self.rope_tables(0, st)
        perm2f = self.sb([128, 128], F32, "perm2f", st)
        perm2 = self.sb([128, 128], BF16, "perm2", st)
        P.dma(perm2f.ap, c_perm2, w=[perm2f], dbuf=perm2f)
        P.copy("dve", perm2, perm2f)
        ones_f = self.sb([128, 128], F32, "onesf", st)
        P.memset("pool", ones_f, 1.0)
        rnw = self.sb([128, 8], F32, "rnw", st)
        rnb = self.sb([128, 8], F32, "rnb", st)
        P.dma(rnw.ap, rn_w[0, :].rearrange("(c p) -> p c", p=128), w=[rnw], dbuf=rnw, allow_slow_non_contiguous=True)
        P.dma(rnb.ap, rn_b[0, :].rearrange("(c p) -> p c", p=128), w=[rnb], dbuf=rnb, allow_slow_non_contiguous=True)
        self.make_rope_sets(st)
        qT = self.sb([128, S], BF16, "rqT", st)
        kT = self.sb([128, S], BF16, "rkT", st)
        kTa = self.sb([128, S], BF16, "rkTa", st)
        kTb = self.sb([128, S], BF16, "rkTb", st)
        P.memset("pool", kTa[64:128, :], 0.0)
        P.memset("pool", kTb[0:64, :], 0.0)
        vt = self.sb([128, NT, 128], BF16, "rvt", st)
        dec = self.sb([128, 4096], BF16, "dec", st)
        pts = [self.sb([128, 512], BF16, f"rpt{i}", st) for i in range(4)]
        oR = self.sb([128, S], F32, "oR", st)
        cen4 = [self.sb([128, 512], F32, f"cen{i}", st) for i in range(4)]
        sqf2 = [self.sb([128, 512], F32, f"sqf{i}", st) for i in range(2)]
        zs = self.sb([128, S], BF16, "zs", st)
        yT = self.sb([128, 2, S], BF16, "yT", st)
        qoff, koff, voff = 3104, 3616, 4128
        for h in range(8):
            if h % 2 == 0:
                for (dst, off) in ((qT, qoff), (kT, koff)):
                    w = self.load_w(w_in[:, off + (h // 2) * 128: off + (h // 2 + 1) * 128]
                                    .rearrange("(k p) c -> p k c", p=128), [128, 8, 128])
                    self.rope_proj4(dst, w, slice(0, 128), 8, None, 128, perm=perm2)
                P.copy("pool", kTa[0:64, :], kT[0:64, :])
                P.copy("pool", kTb[64:128, :], kT[64:128, :])
            w = self.load_w(w_in[:, voff + h * 128: voff + (h + 1) * 128].rearrange("(k p) c -> p k c", p=128),
                            [128, 8, 128])
            for g in range(4):
                pbk = self.pb[g % 2]
                for j in range(4):
                    i = g * 4 + j
                    for k in range(8):
                        P.matmul(pbk[:, j * 128:(j + 1) * 128],
                                 self.xT_tb[i // 4][:, k, (i % 4) * 128:(i % 4 + 1) * 128], w[:, k, :],
                                 start=(k == 0), stop=(k == 7))
                P.copy("dve", V(vt, vt.ap[:, g * 4:(g + 1) * 4, :]), V(pbk, pbk.ap.rearrange("p (a b) -> p a b", a=4)))
            for c in range(4):
                stg = self.wst[self.wst_i % 2]
                self.wst_i += 1
                P.dma(stg.ap[:, 0:1024], c_dec[h, :, c * 1024:(c + 1) * 1024], w=[stg], dbuf=stg)
                P.copy("pool", dec[:, c * 1024:(c + 1) * 1024], stg[:, 0:1024])
            self.silu_z(w_in, zoff + 1024 + h * 128, zs)
            hp = (h % 2) * 64
            steps = [(qb, kt) for qb in range(4) for kt in range(NT)]
            LAG = 3
            sbk = [self.pb[0], self.pb[1], self.pb[2], self.pb[7]]
            for i in range(len(steps) + LAG):
                if i < len(steps):
                    qb, kt = steps[i]
                    qsl = slice(qb * 512, (qb + 1) * 512)
                    ksl = slice(kt * 128, (kt + 1) * 128)
                    sps = sbk[i % 4]
                    pt = pts[i % 4]
                    P.matmul(sps, (kTa if h % 2 == 0 else kTb)[:, ksl], qT[:, qsl])
                    o = qb * 512 - kt * 128 + 2048
                    P.stt(pt, sps, 0.125, dec[:, o:o + 512], ALU.mult, ALU.mult)
                if i >= LAG:
                    qb, kt = steps[i - LAG]
                    pacc = self.pb[3 + (qb % 2)]
                    P.matmul(pacc, vt[:, kt, :], pts[(i - LAG) % 4], start=(kt == 0), stop=(kt == NT - 1))
                    if kt == NT - 1:
                        P.copy("dve", oR[:, qb * 512:(qb + 1) * 512], pacc)
            sl4 = [slice(tb * 512, (tb + 1) * 512) for tb in range(4)]
            rl = [sqf2[0], sqf2[1], self.ztmp[0], self.ztmp[1]]
            for tb in range(4):
                pk = self.pb[5 + tb % 2]
                P.matmul(pk, ones_f, oR[:, sl4[tb]])
                P.stt(cen4[tb], pk, -1.0 / 128.0, oR[:, sl4[tb]], ALU.mult, ALU.add)
            for tb in range(4):
                P.tt("pool", rl[tb], cen4[tb], cen4[tb], ALU.mult)
            for tb in range(4):
                pk2 = self.pb[5 + tb % 2]
                P.matmul(pk2, ones_f, rl[tb])
                P.copy("dve", rl[tb], pk2)
            for tb in range(4):
                P.act(rl[tb], rl[tb], AF.Ln, bias=self.eps_ln, scale=1.0 / 128.0)
            for tb in range(4):
                P.act(rl[tb], rl[tb], AF.Exp, scale=-0.5)
            for tb in range(4):
                P.tt("dve", cen4[tb], cen4[tb], rl[tb], ALU.mult)
                P.ts("dve", cen4[tb], cen4[tb], rnw[:, h:h + 1], rnb[:, h:h + 1], ALU.mult, ALU.add)
                P.tt("pool", yT[:, h % 2, sl4[tb]], cen4[tb], zs[:, sl4[tb]], ALU.mult)
            if h % 2 == 1:
                self.out_proj_pair(yT, w_out, 4 + h // 2)
        P.barrier()
        st.close()
        self.layer_norm_out(ln_w[0:1, :], ln_b[0:1, :], last=True)


def host_consts():
    c = {}
    c["c_ident"] = np.eye(128, dtype=np.float32)
    invf = np.zeros((128, 2), np.float32)
    sgn = np.zeros((128, 2), np.float32)
    f64 = (np.float32(THETA) ** (-np.arange(0, 64, 2, dtype=np.float32) / np.float32(64))).astype(np.float32)
    f128 = (np.float32(THETA) ** (-np.arange(0, 128, 2, dtype=np.float32) / np.float32(128))).astype(np.float32)
    for p in range(128):
        invf[p, 0] = f64[p % 32]
        sgn[p, 0] = -1.0 if (p % 64) < 32 else 1.0
        invf[p, 1] = f128[p % 64]
        sgn[p, 1] = -1.0 if p < 64 else 1.0
    c["c_invf"] = invf
    c["c_sgn"] = sgn
    perm = np.zeros((128, 256), np.float32)
    for i in range(128):
        perm[(i + 64) % 128, i] = 1.0
    for i in range(64):
        perm[(i + 32) % 64, 128 + i] = 1.0
    c["c_perm"] = perm
    dd = np.arange(4096)[None, :] - np.arange(128)[:, None] - 2048
    ad = np.abs(dd)
    m = (ad <= 64).astype(np.float32) + ((dd % 4 == 0) & (ad <= 256)).astype(np.float32) \
        + ((dd % 16 == 0) & (ad <= 1024)).astype(np.float32)
    with np.errstate(divide="ignore"):
        c["c_dilmask"] = np.where(m > 0, np.log(np.maximum(m, 1.0)), -100.0).astype(np.float32)
    i_ = np.arange(128)[:, None]
    j_ = np.arange(128)[None, :]
    trif = (i_ <= j_).astype(np.float32)
    trib = (i_ >= j_).astype(np.float32)
    SL = (i_ > j_).astype(np.float32)
    SU = (i_ < j_).astype(np.float32)
    LI = (i_ >= j_).astype(np.float32)
    UI = (i_ <= j_).astype(np.float32)
    c["c_gmask"] = np.concatenate([trif, trib, SL, SU, LI, UI, -SL, -SU], axis=1).astype(np.float32)
    p2 = np.zeros((128, 128), np.float32)
    for blk in range(2):
        for i in range(64):
            p2[blk * 64 + (i + 32) % 64, blk * 64 + i] = 1.0
    c["c_perm2"] = p2
    dd = np.abs(np.arange(4096)[None, :] - np.arange(128)[:, None] - 2048).astype(np.float32)
    lg = np.log1p(-(np.float32(2.0) ** (-5.0 - np.arange(8, dtype=np.float32)))).astype(np.float32)
    c["c_retdecay"] = np.exp(lg[:, None, None] * dd[None]).astype(np.float32)
    return c


_CACHE = {}


def get_nc(debug_layers=2):
    if debug_layers not in _CACHE:
        nc = bass.Bass("TRN2", target_bir_lowering=False)
        b = Builder(nc, debug_layers)
        b.build()
        _CACHE[debug_layers] = (nc, b)
    return _CACHE[debug_layers][0]


def kernel(debug_layers=2, **inputs):
    import os
    debug_layers = int(os.environ.get("KLAYERS", debug_layers))
    nc = get_nc(debug_layers)
    consts = host_consts()
    x = np.ascontiguousarray(inputs["x"], dtype=np.float32)
    pos = np.ascontiguousarray(inputs["positions"]).astype(np.int32)
    shared = dict(consts)
    if debug_layers == 1:
        for k in ("c_gmask", "c_perm2", "c_retdecay"):
            shared.pop(k)
    for k, v in inputs.items():
        if k in ("x", "positions"):
            continue
        if debug_layers == 1 and k.startswith("odd_"):
            continue
        a = np.ascontiguousarray(v[0], dtype=np.float32)
        if a.ndim == 1:
            a = a[None, :]
        shared[k] = a
    in_maps = []
    for b in range(8):
        m = dict(shared)
        m["x"] = x[b]
        m["pos"] = pos[b:b + 1]
        in_maps.append(m)
    res = run_bass_kernel_spmd(nc, in_maps, core_ids=list(range(8)))
    return np.stack([r["out"] for r in res.results], axis=0).astype(np.float32)
```
